# Optimizing a Trainium2 kernel written in Bass

```python
import math
import jax
import jax.numpy as jnp
from jax import lax
import numpy as np

D_MODEL = 4096
BATCH = 2
SEQ = 8192
DEPTH = 2

BLOCK = 128
MEM_LEN = 256
EPS = 1e-6

A_HEADS = 16
A_Q_RANK = 1024
A_KV_RANK = 512
A_V_DIM = 128
IDX_HEADS = 32
IDX_DIM = 128
TOPK_MAX = 256
B_HEADS = 16
B_DIM = 128
C_HEADS = 32
C_KV_HEADS = 4
C_DIM = 64
WINDOW = 128
D_HEADS = 8
D_QK_DIM = 128
D_V_DIM = 256
X_HEADS = 4
X_DIM = 128
D_FF = 11008
CONV_W = 3
NUM_BUCKETS = 32
MAX_EXACT = 16
MAX_DISTANCE = 128
BIAS_A_OFF = 0
BIAS_C_OFF = A_HEADS
BIAS_D_OFF = A_HEADS + C_HEADS
BIAS_COLS = A_HEADS + C_HEADS + D_HEADS

EVEN_SIZES = (A_Q_RANK, A_KV_RANK, IDX_DIM, IDX_HEADS, B_HEADS * B_DIM, B_HEADS * B_DIM, B_HEADS * B_DIM)
ODD_SIZES = (C_HEADS * C_DIM, C_KV_HEADS * C_DIM, C_KV_HEADS * C_DIM, D_HEADS * 2 * D_QK_DIM, D_HEADS * 2 * D_QK_DIM, D_HEADS * D_V_DIM)
EVEN_IN = sum(EVEN_SIZES)
ODD_IN = sum(ODD_SIZES)
EVEN_OUT = A_HEADS * A_V_DIM + B_HEADS * B_DIM
ODD_OUT = C_HEADS * C_DIM + D_HEADS * D_V_DIM
N_EVEN = (DEPTH + 1) // 2
N_ODD = DEPTH // 2

kernel_name = 'hybrid_dsa_stickbreak_swa_diff_block'


def rmsnorm(x, g):
    xf = x.astype(jnp.float32)
    y = xf * lax.rsqrt(jnp.mean(xf * xf, axis=-1, keepdims=True) + EPS)
    return (y * g.astype(jnp.float32)).astype(x.dtype)


def t5_bucket(dist):
    n = jnp.maximum(dist, 0)
    nf = jnp.maximum(n, 1).astype(jnp.float32)
    large = MAX_EXACT + (jnp.log(nf / MAX_EXACT) / math.log(MAX_DISTANCE / MAX_EXACT) * (NUM_BUCKETS - MAX_EXACT)).astype(jnp.int32)
    return jnp.where(n < MAX_EXACT, n, jnp.minimum(large, NUM_BUCKETS - 1))


def split_cols(x, sizes):
    offs = []
    acc = 0
    for s in sizes[:-1]:
        acc += s
        offs.append(acc)
    return jnp.split(x, offs, axis=-1)


def dsa_attention(q_lat, kv_lat, k_idx, w_idx, q_lat_g, w_qb, q_g, kv_g, w_qi, kidx_g, w_uv, bias_a):
    bsz, seq, _ = q_lat.shape
    topk = min(TOPK_MAX, seq // 4)
    n_blk = seq // BLOCK
    cq = rmsnorm(q_lat, q_lat_g)
    q = rmsnorm(jnp.einsum('bsr,rhc->bshc', cq, w_qb), q_g)
    qi = jnp.einsum('bsr,rhd->bshd', cq, w_qi)
    c = rmsnorm(kv_lat, kv_g)
    ki = rmsnorm(k_idx, kidx_g)
    wi = w_idx * (IDX_HEADS ** -0.5 * IDX_DIM ** -0.5)
    key_pos = jnp.arange(seq, dtype=jnp.int32)

    def block(i):
        start = i * BLOCK
        t = start + jnp.arange(BLOCK, dtype=jnp.int32)
        qb = lax.dynamic_slice_in_dim(q, start, BLOCK, axis=1)
        qib = lax.dynamic_slice_in_dim(qi, start, BLOCK, axis=1)
        wib = lax.dynamic_slice_in_dim(wi, start, BLOCK, axis=1)
        rel = jax.nn.relu(jnp.einsum('bthd,bsd->bths', qib, ki))
        score = jnp.einsum('bths,bth->bts', rel, wib).astype(jnp.float32)
        score = jnp.where((key_pos[None, :] <= t[:, None])[None], score, -jnp.inf)
        _, sel = lax.top_k(score, topk)
        valid = sel <= t[None, :, None]
        cg = jax.vmap(lambda cb, ib: cb[ib])(c, sel)
        logits = jnp.einsum('bthr,btkr->bhtk', qb, cg).astype(jnp.float32) * A_KV_RANK ** -0.5
        bias = bias_a[t5_bucket(t[None, :, None] - sel)].astype(jnp.float32)
        logits = jnp.where(valid[:, None], logits + jnp.moveaxis(bias, -1, 1), -jnp.inf)
        p = jax.nn.softmax(logits, axis=-1).astype(cg.dtype)
        o_lat = jnp.einsum('bhtk,btkr->bthr', p, cg)
        return jnp.einsum('bthr,hrd->bthd', o_lat, w_uv)

    out = lax.map(block, jnp.arange(n_blk))
    return jnp.swapaxes(out, 0, 1).reshape(bsz, seq, A_HEADS * A_V_DIM)


def stick_breaking_attention(q, k, v):
    bsz, seq, nh, dh = q.shape
    n_blk = seq // BLOCK
    key_pos = jnp.arange(seq, dtype=jnp.int32)

    def block(i):
        start = i * BLOCK
        t = start + jnp.arange(BLOCK, dtype=jnp.int32)
        qb = lax.dynamic_slice_in_dim(q, start, BLOCK, axis=1)
        z = jnp.einsum('bthd,bshd->bhts', qb, k).astype(jnp.float32) * dh ** -0.5
        strict = (key_pos[None, :] < t[:, None])[None, None]
        log1m = jnp.where(strict, jax.nn.log_sigmoid(-z), 0.0)
        between = lax.cumsum(log1m, axis=3, reverse=True) - log1m
        w = jnp.where(strict, jnp.exp(jax.nn.log_sigmoid(z) + between), 0.0).astype(v.dtype)
        return jnp.einsum('bhts,bshd->bthd', w, v)

    out = lax.map(block, jnp.arange(n_blk))
    return jnp.swapaxes(out, 0, 1).reshape(bsz, seq, nh * dh)


def swa_sink_attention(q, k, v, sinks, bias_c):
    bsz, seq, nh, dh = q.shape
    n_blk = seq // BLOCK
    grp = nh // C_KV_HEADS
    qb = q.reshape(bsz, n_blk, BLOCK, C_KV_HEADS, grp, dh)

    def band(a):
        ab = a.reshape(bsz, n_blk, BLOCK, C_KV_HEADS, dh)
        prev = jnp.pad(ab, ((0, 0), (1, 0), (0, 0), (0, 0), (0, 0)))[:, :-1]
        return jnp.concatenate([prev, ab], axis=2)

    kb = band(k)
    vb = band(v)
    logits = jnp.einsum('bnqhgd,bnshd->bnhgqs', qb, kb).astype(jnp.float32) * dh ** -0.5
    qa = jnp.arange(BLOCK, dtype=jnp.int32)
    sa = jnp.arange(2 * BLOCK, dtype=jnp.int32)
    rel = qa[:, None] + BLOCK - sa[None, :]
    in_win = (rel >= 0) & (rel < WINDOW)
    blk_ok = (jnp.arange(n_blk)[:, None] > 0) | (sa[None, :] >= BLOCK)
    mask = in_win[None] & blk_ok[:, None, :]
    bias = jnp.moveaxis(bias_c[t5_bucket(rel)], -1, 0).reshape(C_KV_HEADS, grp, BLOCK, 2 * BLOCK)
    logits = jnp.where(mask[None, :, None, None], logits + bias.astype(jnp.float32), -jnp.inf)
    sink = sinks.astype(jnp.float32).reshape(1, 1, C_KV_HEADS, grp, 1, 1)
    m = jnp.maximum(jnp.max(logits, axis=-1, keepdims=True), sink)
    e = jnp.exp(logits - m)
    p = (e / (jnp.sum(e, axis=-1, keepdims=True) + jnp.exp(sink - m))).astype(v.dtype)
    o = jnp.einsum('bnhgqs,bnshd->bnqhgd', p, vb)
    return o.reshape(bsz, seq, nh * dh)


def differential_attention(q, k, v, lam_q1, lam_k1, lam_q2, lam_k2, sub_g, bias_d, lambda_init):
    bsz, seq, nh, _, dq = q.shape
    n_blk = seq // BLOCK
    key_pos = jnp.arange(seq, dtype=jnp.int32)
    lam = (jnp.exp(jnp.sum(lam_q1 * lam_k1)) - jnp.exp(jnp.sum(lam_q2 * lam_k2))).astype(jnp.float32) + lambda_init

    def block(i):
        start = i * BLOCK
        t = start + jnp.arange(BLOCK, dtype=jnp.int32)
        qb = lax.dynamic_slice_in_dim(q, start, BLOCK, axis=1)
        logits = jnp.einsum('bthcd,bshcd->bchts', qb, k).astype(jnp.float32) * dq ** -0.5
        dist = t[:, None] - key_pos[None, :]
        bias = jnp.moveaxis(bias_d[t5_bucket(dist)], -1, 0).astype(jnp.float32)
        logits = jnp.where((dist >= 0)[None, None, None], logits + bias, -jnp.inf)
        p = jax.nn.softmax(logits, axis=-1)
        w = (p[:, 0] - lam * p[:, 1]).astype(v.dtype)
        return jnp.einsum('bhts,bshd->bthd', w, v)

    out = jnp.swapaxes(lax.map(block, jnp.arange(n_blk)), 0, 1).reshape(bsz, seq, nh, v.shape[-1])
    out = rmsnorm(out, sub_g) * (1.0 - lambda_init)
    return out.reshape(bsz, seq, nh * v.shape[-1])


def even_mixer(h, w_in, q_lat_g, w_qb, q_g, kv_g, w_uv, w_qi, kidx_g, w_out, rel_bias):
    bsz, seq, _ = h.shape
    proj = h @ w_in
    a_qlat, a_kv, a_kidx, a_widx, b_q, b_k, b_v = split_cols(proj, EVEN_SIZES)
    o_a = dsa_attention(a_qlat, a_kv, a_kidx, a_widx, q_lat_g, w_qb, q_g, kv_g, w_qi, kidx_g, w_uv,
                        rel_bias[:, BIAS_A_OFF:BIAS_A_OFF + A_HEADS])
    shp = (bsz, seq, B_HEADS, B_DIM)
    o_b = stick_breaking_attention(b_q.reshape(shp), b_k.reshape(shp), b_v.reshape(shp))
    return jnp.concatenate([o_a, o_b], axis=-1) @ w_out


def odd_mixer(h, w_in, c_q_g, c_k_g, sinks, d_q_g, d_k_g, lam_q1, lam_k1, lam_q2, lam_k2, sub_g, w_out, rel_bias, lambda_init):
    bsz, seq, _ = h.shape
    proj = h @ w_in
    c_q, c_k, c_v, d_q, d_k, d_v = split_cols(proj, ODD_SIZES)
    cq = rmsnorm(c_q.reshape(bsz, seq, C_HEADS, C_DIM), c_q_g)
    ck = rmsnorm(c_k.reshape(bsz, seq, C_KV_HEADS, C_DIM), c_k_g)
    cv = c_v.reshape(bsz, seq, C_KV_HEADS, C_DIM)
    o_c = swa_sink_attention(cq, ck, cv, sinks, rel_bias[:, BIAS_C_OFF:BIAS_C_OFF + C_HEADS])
    dq = rmsnorm(d_q.reshape(bsz, seq, D_HEADS, 2, D_QK_DIM), d_q_g)
    dk = rmsnorm(d_k.reshape(bsz, seq, D_HEADS, 2, D_QK_DIM), d_k_g)
    dv = d_v.reshape(bsz, seq, D_HEADS, D_V_DIM)
    o_d = differential_attention(dq, dk, dv, lam_q1, lam_k1, lam_q2, lam_k2, sub_g,
                                 rel_bias[:, BIAS_D_OFF:BIAS_D_OFF + D_HEADS], lambda_init)
    return jnp.concatenate([o_c, o_d], axis=-1) @ w_out


def cross_attention(h, memn, wq, wk, wv, q_g, k_g, wo):
    bsz, seq, _ = h.shape
    q = rmsnorm(jnp.einsum('bsd,dhc->bshc', h, wq), q_g)
    k = rmsnorm(jnp.einsum('bmd,dhc->bmhc', memn, wk), k_g)
    v = jnp.einsum('bmd,dhc->bmhc', memn, wv)
    logits = jnp.einsum('bshc,bmhc->bhsm', q, k).astype(jnp.float32) * X_DIM ** -0.5
    p = jax.nn.softmax(logits, axis=-1).astype(v.dtype)
    o = jnp.einsum('bhsm,bmhc->bshc', p, v).reshape(bsz, seq, X_HEADS * X_DIM)
    return o @ wo


def conv_ffn(h, w_gate, w_up, conv_w, conv_b, w_down):
    g = h @ w_gate
    g = lax.conv_general_dilated(g, conv_w[:, None, :], window_strides=(1,), padding=[(CONV_W - 1, 0)],
                                 dimension_numbers=('NWC', 'WIO', 'NWC'), feature_group_count=D_FF) + conv_b
    return (jax.nn.silu(g) * (h @ w_up)) @ w_down


def setup_inputs(seed: int = 0) -> dict:
    key = jax.random.key(seed)
    ks = iter(jax.random.split(key, 48))

    def nrm(shape, scale):
        return jax.random.normal(next(ks), shape, jnp.float32) * scale

    def gain(shape):
        return 1.0 + nrm(shape, 0.02)

    d = D_MODEL
    return {
        'x': nrm((BATCH, SEQ, d), 1.0),
        'mem': nrm((BATCH, MEM_LEN, d), 1.0),
        'rel_bias': nrm((NUM_BUCKETS, BIAS_COLS), 0.2),
        'mem_norm_g': gain((d,)),
        'mix_norm_g': gain((DEPTH, d)),
        'xattn_norm_g': gain((DEPTH, d)),
        'ffn_norm_g': gain((DEPTH, d)),
        'ev_w_in': nrm((N_EVEN, d, EVEN_IN), d ** -0.5),
        'ev_q_lat_g': gain((N_EVEN, A_Q_RANK)),
        'ev_w_qb': nrm((N_EVEN, A_Q_RANK, A_HEADS, A_KV_RANK), A_Q_RANK ** -0.5),
        'ev_q_g': gain((N_EVEN, A_KV_RANK)),
        'ev_kv_g': gain((N_EVEN, A_KV_RANK)),
        'ev_w_uv': nrm((N_EVEN, A_HEADS, A_KV_RANK, A_V_DIM), A_KV_RANK ** -0.5),
        'ev_w_qi': nrm((N_EVEN, A_Q_RANK, IDX_HEADS, IDX_DIM), A_Q_RANK ** -0.5),
        'ev_kidx_g': gain((N_EVEN, IDX_DIM)),
        'ev_w_out': nrm((N_EVEN, EVEN_OUT, d), EVEN_OUT ** -0.5),
        'od_w_in': nrm((N_ODD, d, ODD_IN), d ** -0.5),
        'od_c_q_g': gain((N_ODD, C_DIM)),
        'od_c_k_g': gain((N_ODD, C_DIM)),
        'od_sinks': nrm((N_ODD, C_HEADS), 0.5),
        'od_d_q_g': gain((N_ODD, D_QK_DIM)),
        'od_d_k_g': gain((N_ODD, D_QK_DIM)),
        'od_lam_q1': nrm((N_ODD, D_QK_DIM), 0.1),
        'od_lam_k1': nrm((N_ODD, D_QK_DIM), 0.1),
        'od_lam_q2': nrm((N_ODD, D_QK_DIM), 0.1),
        'od_lam_k2': nrm((N_ODD, D_QK_DIM), 0.1),
        'od_sub_g': gain((N_ODD, D_V_DIM)),
        'od_w_out': nrm((N_ODD, ODD_OUT, d), ODD_OUT ** -0.5),
        'x_wq': nrm((DEPTH, d, X_HEADS, X_DIM), d ** -0.5),
        'x_wk': nrm((DEPTH, d, X_HEADS, X_DIM), d ** -0.5),
        'x_wv': nrm((DEPTH, d, X_HEADS, X_DIM), d ** -0.5),
        'x_q_g': gain((DEPTH, X_DIM)),
        'x_k_g': gain((DEPTH, X_DIM)),
        'x_wo': nrm((DEPTH, X_HEADS * X_DIM, d), (X_HEADS * X_DIM) ** -0.5),
        'f_w_gate': nrm((DEPTH, d, D_FF), d ** -0.5),
        'f_w_up': nrm((DEPTH, d, D_FF), d ** -0.5),
        'f_conv_w': nrm((DEPTH, CONV_W, D_FF), CONV_W ** -0.5),
        'f_conv_b': nrm((DEPTH, D_FF), 0.01),
        'f_w_down': nrm((DEPTH, D_FF, d), D_FF ** -0.5),
    }


def reference(x, mem, rel_bias, mem_norm_g, mix_norm_g, xattn_norm_g, ffn_norm_g,
              ev_w_in, ev_q_lat_g, ev_w_qb, ev_q_g, ev_kv_g, ev_w_uv, ev_w_qi, ev_kidx_g, ev_w_out,
              od_w_in, od_c_q_g, od_c_k_g, od_sinks, od_d_q_g, od_d_k_g, od_lam_q1, od_lam_k1, od_lam_q2, od_lam_k2,
              od_sub_g, od_w_out,
              x_wq, x_wk, x_wv, x_q_g, x_k_g, x_wo,
              f_w_gate, f_w_up, f_conv_w, f_conv_b, f_w_down):
    memn = rmsnorm(mem, mem_norm_g)
    for l in range(DEPTH):
        h = rmsnorm(x, mix_norm_g[l])
        if l % 2 == 0:
            e = l // 2
            x = x + even_mixer(h, ev_w_in[e], ev_q_lat_g[e], ev_w_qb[e], ev_q_g[e], ev_kv_g[e], ev_w_uv[e],
                               ev_w_qi[e], ev_kidx_g[e], ev_w_out[e], rel_bias)
        else:
            o = l // 2
            lambda_init = 0.8 - 0.6 * math.exp(-0.3 * l)
            x = x + odd_mixer(h, od_w_in[o], od_c_q_g[o], od_c_k_g[o], od_sinks[o], od_d_q_g[o], od_d_k_g[o],
                              od_lam_q1[o], od_lam_k1[o], od_lam_q2[o], od_lam_k2[o], od_sub_g[o], od_w_out[o],
                              rel_bias, lambda_init)
        x = x + cross_attention(rmsnorm(x, xattn_norm_g[l]), memn, x_wq[l], x_wk[l], x_wv[l], x_q_g[l], x_k_g[l], x_wo[l])
        x = x + conv_ffn(rmsnorm(x, ffn_norm_g[l]), f_w_gate[l], f_w_up[l], f_conv_w[l], f_conv_b[l], f_w_down[l])
    return x
```

```python
import math
import os
from contextlib import ExitStack
import numpy as np
import concourse.bass as bass
import concourse.mybir as mybir
from concourse.bass_utils import run_bass_kernel_spmd

F32 = mybir.dt.float32
BF16 = mybir.dt.bfloat16
AF = mybir.ActivationFunctionType
ALU = mybir.AluOpType
AX = mybir.AxisListType

ENGS = ("pe", "act", "dve", "pool", "sp")
NDMASEM = 6
NEG = -30000.0
NOCC = bool(os.environ.get("NOCC"))
EPS = 1e-6


class Buf:
    __slots__ = ("name", "lw", "rd")

    def __init__(self, name=""):
        self.name = name
        self.lw = None
        self.rd = []


class Prog:
    def __init__(self, nc, stack):
        self.nc = nc
        self.stack = stack
        self.ops = {e: [] for e in ENGS}
        self.sem = {e: stack.enter_context(nc.semaphore("s_" + e)) for e in ENGS}
        self.count = {e: 0 for e in ENGS}
        self.dsem = {q: [stack.enter_context(nc.semaphore(f"d_{q}{i}")) for i in range(NDMASEM)] for q in ("sp", "pool", "act")}
        self.dcount = {q: 0 for q in ("sp", "pool", "act")}
        self.seen = {e: {} for e in ENGS}
        self.ncc = 0
        self.ccsems = [stack.enter_context(nc.semaphore(f"cc{i}")) for i in range(int(os.environ.get("NCC", "70")))]
        allsems = list(self.sem.values()) + [s_ for q in self.dsem.values() for s_ in q] + self.ccsems
        if os.environ.get("CLR"):
            with nc.Block() as blk:
                def clr(g):
                    for s_ in allsems:
                        g.sem_clear(s_)
                blk.gpsimd(clr)
            nc.all_engine_barrier()

    def _need(self, eng, tokens):
        best = {}
        for tok in tokens:
            sem, val, key, src = tok
            if src == eng and eng == "pe":
                continue
            if self.seen[eng].get(key, 0) >= val:
                continue
            self.seen[eng][key] = val
            if key not in best or best[key][1] < val:
                best[key] = (sem, val)
        return list(best.values())

    @staticmethod
    def _deps(reads, writes):
        toks = []
        for r in reads:
            if r.lw is not None:
                toks.append(r.lw)
        for w in writes:
            if w.lw is not None:
                toks.append(w.lw)
            toks.extend(w.rd)
        return toks

    @staticmethod
    def _upd(tok, reads, writes):
        for r in reads:
            r.rd.append(tok)
            if len(r.rd) > 64:
                r.rd = r.rd[-48:]
        for w in writes:
            w.lw = tok
            w.rd = []

    def op(self, eng, fn, reads=(), writes=()):
        waits = self._need(eng, self._deps(reads, writes))
        self.count[eng] += 1
        tok = (self.sem[eng], self.count[eng], "e_" + eng, eng)
        self.ops[eng].append((fn, waits, (self.sem[eng], 1)))
        self._upd(tok, reads, writes)
        return tok

    def dma(self, q, fn, reads=(), writes=()):
        toks = self._deps(reads, writes)
        n = self.dcount[q]
        self.dcount[q] += 1
        slot, gen = n % NDMASEM, n // NDMASEM
        sem = self.dsem[q][slot]
        key = f"d_{q}{slot}"
        if gen > 0:
            toks.append((sem, 16 * gen, key, "dma"))
        waits = self._need(q, toks)
        tok = (sem, 16 * (gen + 1), key, "dma")
        self.ops[q].append((fn, waits, (sem, 16)))
        self._upd(tok, reads, writes)
        return tok

    def cc(self, fn, reads=(), writes=()):
        toks = self._deps(reads, writes)
        if getattr(self, "last_cc", None) is not None:
            toks.append(self.last_cc)
        waits = self._need("pool", toks)
        sem = self.ccsems[self.ncc]
        self.ncc += 1
        tok = (sem, 1, f"cc{self.ncc}", "cc")
        self.ops["pool"].append((fn, waits, (sem, None)))
        self._upd(tok, reads, writes)
        self.last_cc = tok
        return tok

    def end_phase(self):
        toks = []
        for q in ("sp", "pool", "act"):
            n = self.dcount[q]
            for slot in range(NDMASEM):
                if n > slot:
                    gens = (n - 1 - slot) // NDMASEM + 1
                    toks.append((self.dsem[q][slot], 16 * gens, f"d_{q}{slot}", "dma"))
        if getattr(self, "last_cc", None) is not None:
            toks.append(self.last_cc)
        self.ops["sp"].append((None, self._need("sp", toks), None))
        self.emit()

    def final_wait(self, eng, bufs):
        toks = [b.lw for b in bufs if b.lw is not None]
        self.ops[eng].append((None, self._need(eng, toks), None))

    def emit(self):
        nc = self.nc
        ops = self.ops
        self.ops = {e: [] for e in ENGS}
        with nc.Block() as block:
            def run(e):
                def body(engine):
                    for fn, waits, inc in ops[e]:
                        for sem, val in waits:
                            engine.wait_ge(sem, val)
                        if fn is None:
                            continue
                        ins = fn(engine)
                        if inc is not None:
                            if inc[1] is None:
                                ins.then_inc(inc[0])
                            else:
                                ins.then_inc(inc[0], inc[1])
                return body
            block.tensor(run("pe"))
            block.scalar(run("act"))
            block.vector(run("dve"))
            block.gpsimd(run("pool"))
            block.sync(run("sp"))


class Ring:
    def __init__(self, tiles):
        self.tiles = tiles
        self.bufs = [Buf() for _ in tiles]
        self.i = 0

    def next(self):
        k = self.i % len(self.tiles)
        self.i += 1
        return self.tiles[k], self.bufs[k]


class Cfg:
    def __init__(self, **kw):
        self.__dict__.update(kw)
        s = self
        s.NBLK = s.S // 128
        s.MB = s.NBLK // 4
        s.TL = s.MB * 128
        s.T = s.TG * 128
        s.NG = s.MB // s.TG
        s.KC = s.D // 128
        s.EVEN_SIZES = (s.A_QR, s.A_KVR, s.I_D, s.I_H, s.B_H * s.B_D, s.B_H * s.B_D, s.B_H * s.B_D)
        s.ODD_SIZES = (s.C_H * s.C_D, s.C_KV * s.C_D, s.C_KV * s.C_D, s.D_H * 2 * s.D_QK, s.D_H * 2 * s.D_QK, s.D_H * s.D_DV)
        s.EVEN_OUT = s.A_H * s.A_DV + s.B_H * s.B_D
        s.ODD_OUT = s.C_H * s.C_D + s.D_H * s.D_DV
        s.TOPK = min(256, s.S // 4)
        s.BIAS_COLS = s.A_H + s.C_H + s.D_H


FULL = Cfg(D=4096, B=2, S=8192, MEM=256, A_H=16, A_QR=1024, A_KVR=512, A_DV=128, I_H=32, I_D=128,
           B_H=16, B_D=128, C_H=32, C_KV=4, C_D=64, D_H=8, D_QK=128, D_DV=256, X_H=4, X_D=128, FF=11008, TG=4)

def big_weights(c):
    return {
        "ev_w_in": (c.D, sum(c.EVEN_SIZES)), "ev_w_qb": (c.A_QR, c.A_H * c.A_KVR), "ev_w_qi": (c.A_QR, c.I_H * c.I_D),
        "ev_w_uv": (c.A_H * c.A_KVR, c.A_DV), "ev_w_out": (c.EVEN_OUT, c.D),
        "od_w_in": (c.D, sum(c.ODD_SIZES)), "od_w_out": (c.ODD_OUT, c.D),
        "x_wq0": (c.D, c.X_H * c.X_D), "x_wk0": (c.D, c.X_H * c.X_D), "x_wv0": (c.D, c.X_H * c.X_D), "x_wo0": (c.X_H * c.X_D, c.D),
        "x_wq1": (c.D, c.X_H * c.X_D), "x_wk1": (c.D, c.X_H * c.X_D), "x_wv1": (c.D, c.X_H * c.X_D), "x_wo1": (c.X_H * c.X_D, c.D),
        "f_w_gate0": (c.D, c.FF), "f_w_up0": (c.D, c.FF), "f_w_down0": (c.FF, c.D),
        "f_w_gate1": (c.D, c.FF), "f_w_up1": (c.D, c.FF), "f_w_down1": (c.FF, c.D),
    }


class K:
    def __init__(self, cfg, debug=()):
        self.c = cfg
        self.debug = set(debug)
        self.nc = bass.Bass("TRN2", target_bir_lowering=False)
        self.stack = ExitStack()
        self.P = Prog(self.nc, self.stack)
        self.dram = {}
        self.inputs = []
        self.outputs = []
        self.shapes = {}
        self.gpieces = {}

    def ext_in(self, name, shape, dt=F32):
        if name in self.dram:
            return self.dram[name][0]
        t = self.nc.dram_tensor(name, list(shape), dt, kind="ExternalInput")
        self.dram[name] = (t, Buf(name))
        self.inputs.append(name)
        return t

    def scratch(self, name, shape, dt=BF16):
        t = self.nc.dram_tensor(name, list(shape), dt, kind="Internal")
        self.dram[name] = (t, Buf(name))
        self.shapes[name] = (list(shape), dt)
        return t

    def dump_debug(self):
        outs = []
        for name in self.debug:
            shape, dt = self.shapes[name]
            o = self.ext_out("o_" + name, shape, dt)
            src = self.d(name)
            rows = shape[0]
            step = max(1, min(rows, (1 << 20) // max(1, (int(np.prod(shape[1:])) * 2))))
            for r0 in range(0, rows, step):
                r1 = min(rows, r0 + step)
                self.P.dma("pool", lambda e, r0=r0, r1=r1, o=o, src=src: e.dma_start(out=o[r0:r1], in_=src[r0:r1]), reads=[self.db(name)], writes=[self.db("o_" + name)])
            outs.append(self.db("o_" + name))
        self.P.final_wait("pool", outs)

    def ext_out(self, name, shape, dt=F32):
        t = self.nc.dram_tensor(name, list(shape), dt, kind="ExternalOutput")
        self.dram[name] = (t, Buf(name))
        self.outputs.append(name)
        return t

    def d(self, name):
        return self.dram[name][0]

    def db(self, name):
        return self.dram[name][1]

    def sb(self, ctx, name, shape, dt):
        return ctx.enter_context(self.nc.sbuf_tensor(name, list(shape), dt))

    def ps(self, ctx, name, shape, dt=F32):
        return ctx.enter_context(self.nc.psum_tensor(name, list(shape), dt))


def rstd_op(P, out_ap, in_ap, dim, reads, wbuf):
    P.op("act", lambda e: e.activation(out=out_ap, in_=in_ap, func=AF.Sqrt, scale=1.0 / dim, bias=EPSB[0][0:out_ap.shape[0], :]), reads=list(reads) + [EPSB[1]], writes=[wbuf])
    P.op("dve", lambda e: e.reciprocal(out=out_ap, in_=out_ap), reads=[wbuf], writes=[wbuf])


EPSB = [None, None]


def dma3(P, q, out_ap, in_ap, reads, writes, step=8):
    X = out_ap.shape[1]
    for x0 in range(0, X, step):
        x1 = min(X, x0 + step)
        P.dma(q, lambda e, x0=x0, x1=x1: e.dma_start(out=out_ap[:, x0:x1], in_=in_ap[:, x0:x1]), reads=reads, writes=writes)


def cdiv(a, b):
    return (a + b - 1) // b


def phase_consts(k, ctx):
    nc, P = k.nc, k.P
    k.ident = k.sb(ctx, "ident", [128, 128], BF16)
    k.ident4 = k.sb(ctx, "ident4", [128, 4, 128], BF16)
    k.ones = k.sb(ctx, "ones", [128, 128], BF16)
    k.ones_half = k.sb(ctx, "ones_half", [128, 128], BF16)
    k.zeros = k.sb(ctx, "zeros", [128, 512], BF16)
    k.identf = k.sb(ctx, "identf", [128, 128], F32)
    k.cb = Buf("consts")
    k.epst = k.sb(ctx, "epst", [128, 1], F32)
    EPSB[0] = k.epst
    EPSB[1] = k.cb
    P.op("dve", lambda e: e.memset(k.epst[:], EPS), writes=[k.cb])
    idn = k.ext_in("c_ident", [128, 128])
    P.dma("sp", lambda e: e.dma_start(out=k.identf[:], in_=idn[:, :]), writes=[k.cb])
    P.op("dve", lambda e: e.tensor_copy(out=k.ident[:], in_=k.identf[:]), reads=[k.cb], writes=[k.cb])
    for i in range(4):
        P.op("dve", lambda e, i=i: e.tensor_copy(out=k.ident4[:, i, :], in_=k.identf[:]), reads=[k.cb], writes=[k.cb])
    P.op("dve", lambda e: e.memset(k.ones[:], 1.0), writes=[k.cb])
    P.op("dve", lambda e: e.memset(k.zeros[:], 0.0), writes=[k.cb])
    P.op("dve", lambda e: e.memset(k.ones_half[:], 0.0), writes=[k.cb])
    P.op("dve", lambda e: e.memset(k.ones_half[0:64, 0:64], 1.0), writes=[k.cb])
    P.op("dve", lambda e: e.memset(k.ones_half[64:128, 64:128], 1.0), writes=[k.cb])


def phase_weights(k):
    c, nc, P = k.c, k.nc, k.P
    rg = [list(range(8))]
    for name, (Kr, N) in big_weights(c).items():
        assert Kr % 8 == 0
        rs = Kr // 8
        if NOCC:
            src = k.ext_in(name, [Kr, N])
            full = k.scratch("w_" + name, [Kr, N])
            rows_per = max(1, min(Kr, (2 << 20) // (N * 4)))
            for r0 in range(0, Kr, rows_per):
                r1 = min(Kr, r0 + rows_per)
                P.dma("pool", lambda e, r0=r0, r1=r1, full=full, src=src: e.dma_start(out=full[r0:r1, :], in_=src[r0:r1, :]), writes=[k.db("w_" + name)])
            continue
        src = k.ext_in(name, [rs, N])
        bnc = k.scratch("bn_" + name, [rs, N])
        full = k.scratch("w_" + name, [Kr, N])
        rows_per = max(1, min(rs, (2 << 20) // (N * 4)))
        for r0 in range(0, rs, rows_per):
            r1 = min(rs, r0 + rows_per)
            P.dma("pool", lambda e, r0=r0, r1=r1, bnc=bnc, src=src: e.dma_start(out=bnc[r0:r1, :], in_=src[r0:r1, :]),
                  writes=[k.db("bn_" + name)])
        P.cc(lambda e, bnc=bnc, full=full: e.collective_compute("AllGather", ALU.bypass, replica_groups=rg,
                                                                 ins=[bnc.ap().opt()], outs=[full.ap().opt()]),
             reads=[k.db("bn_" + name)], writes=[k.db("w_" + name)])


GMAX = int(os.environ.get("GMAX", str(1 << 20)))


def gather4(k, name, shape, dt=BF16):
    P = k.P
    src = k.d(name)
    R, C = shape
    k.gpieces[name] = (R, 1)
    pn = f"g_{name}_0"
    full = k.scratch(pn, [4 * R, C], dt)
    if NOCC:
        for rk in range(4):
            P.dma("pool", lambda e, rk=rk: e.dma_start(out=full[rk * R:(rk + 1) * R, :], in_=src[:, :]), reads=[k.db(name)], writes=[k.db(pn)])
        return
    g8n = f"g8_{name}"
    g8 = k.scratch(g8n, [8 * R, C], dt)
    rg = [list(range(8))]
    P.cc(lambda e: e.collective_compute("AllGather", ALU.bypass, replica_groups=rg, ins=[src.ap().opt()], outs=[g8.ap().opt()]),
         reads=[k.db(name)], writes=[k.db(g8n)])
    s0, s1 = ppc(k, "bsel0"), ppc(k, "bsel1")
    with ExitStack() as ctx:
        tg = f"gb_{name}"
        ta = Ring([k.sb(ctx, f"{tg}a{i}", [128, C], dt) for i in range(2)])
        tb_ = Ring([k.sb(ctx, f"{tg}b{i}", [128, C], dt) for i in range(2)])
        for n in range(4):
            for r0 in range(0, R, 128):
                nr_ = min(128, R - r0)
                a_, ab_ = ta.next()
                b_, bb_ = tb_.next()
                P.dma("sp", lambda e, a_=a_, n=n, r0=r0, nr_=nr_: e.dma_start(out=a_[0:nr_, :], in_=g8[n * R + r0:n * R + r0 + nr_, :]), reads=[k.db(g8n)], writes=[ab_])
                P.dma("sp", lambda e, b_=b_, n=n, r0=r0, nr_=nr_: e.dma_start(out=b_[0:nr_, :], in_=g8[(4 + n) * R + r0:(4 + n) * R + r0 + nr_, :]), reads=[k.db(g8n)], writes=[bb_])
                P.op("dve", lambda e, a_=a_, nr_=nr_: e.tensor_scalar(out=a_[0:nr_, :], in0=a_[0:nr_, :], scalar1=s0[0:nr_, :], scalar2=None, op0=ALU.mult), reads=[ab_, k.cb], writes=[ab_])
                P.op("dve", lambda e, a_=a_, b_=b_, nr_=nr_: e.scalar_tensor_tensor(out=a_[0:nr_, :], in0=b_[0:nr_, :], scalar=s1[0:nr_, :], in1=a_[0:nr_, :], op0=ALU.mult, op1=ALU.add),
                     reads=[ab_, bb_, k.cb], writes=[ab_])
                P.dma("pool", lambda e, a_=a_, n=n, r0=r0, nr_=nr_: e.dma_start(out=full[n * R + r0:n * R + r0 + nr_, :], in_=a_[0:nr_, :]), reads=[ab_], writes=[k.db(pn)])
        P.end_phase()


def gsegs(k, name, rank, r0, nrows):
    Rp, npc = k.gpieces[name]
    out = []
    r = r0
    while r < r0 + nrows:
        p = r // Rp
        lo = r - p * Rp
        n = min(Rp - lo, r0 + nrows - r)
        pn = f"g_{name}_{p}"
        out.append((k.d(pn)[rank * Rp + lo:rank * Rp + lo + n, :], k.db(pn), r - r0, n))
        r += n
    return out


class NormCtx:
    pass


def alloc_proj(k, ctx, T, KCmax, tag, PW=512, RAWC=8):
    r = NormCtx()
    r.T = T
    r.PW = PW
    r.wp = Ring([k.sb(ctx, f"{tag}_wp{i}", [128, KCmax, PW], BF16) for i in range(2)])
    r.pacc = Ring([k.ps(ctx, f"{tag}_pa{i}", [128, 512]) for i in range(2)])
    r.pssq = Ring([k.ps(ctx, f"{tag}_pq{i}", [128, 512]) for i in range(1)])
    r.raw = k.sb(ctx, f"{tag}_raw", [128, RAWC, T], F32)
    r.rawb = Buf()
    r.sq = Ring([k.sb(ctx, f"{tag}_sq{i}", [128, T], BF16) for i in range(2)])
    r.rstd = k.sb(ctx, f"{tag}_rstd", [128, T], F32)
    r.rstdb = Buf()
    r.stg = Ring([k.sb(ctx, f"{tag}_stg{i}", [128, 512], BF16) for i in range(4)])
    r.stgf = Ring([k.sb(ctx, f"{tag}_stf{i}", [128, 512], F32) for i in range(3)])
    r.small = Ring([k.sb(ctx, f"{tag}_sm{i}", [128, 4], F32) for i in range(4)])
    r.flip = 0
    return r


def load_panel(k, r, W, wbuf, KCin, c0, n, colmap=None):
    P = k.P
    wp, wb = r.wp.next()
    if colmap is None:
        src = W[0:KCin * 128, c0:c0 + n].rearrange("(kc p) n -> p kc n", p=128)
        dma3(P, "sp", wp[:, 0:KCin, 0:n], src, [wbuf], [wb])
    else:
        for (dc, sc, w) in colmap:
            src = W[0:KCin * 128, sc:sc + w].rearrange("(kc p) n -> p kc n", p=128)
            dma3(P, "sp", wp[:, 0:KCin, dc:dc + w], src, [wbuf], [wb])
    return wp, wb


def fm_heads(k, r, hT, hb, KCin, T, W, wbuf, col0, nheads, cph, norm, gs, out_cb, dim=None, colmap_fn=None):
    P = k.P
    nch = nheads * cph
    panel = None
    for ci in range(nch):
        h, cc = divmod(ci, cph)
        CPP = r.PW // 128
        if ci % CPP == 0:
            n = min(CPP, nch - ci) * 128
            cm = colmap_fn(ci, n) if colmap_fn else None
            panel = load_panel(k, r, W, wbuf, KCin, col0 + ci * 128, n, cm)
        wp, wb = panel
        pc = (ci % CPP) * 128
        acc, ab = r.pacc.next()
        for kc in range(KCin):
            P.op("pe", lambda e, acc=acc, wp=wp, kc=kc, pc=pc: e.matmul(acc[:, 0:T], wp[:, kc, pc:pc + 128], hT[:, kc, 0:T],
                                                                      start=(kc == 0), stop=(kc == KCin - 1)),
                 reads=[wb, hb], writes=[ab])
        if norm is None:
            st, sbf = r.stg.next()
            if isinstance(gs, float):
                if r.flip % 2 == 0:
                    P.op("act", lambda e, st=st, acc=acc: e.activation(out=st[:, 0:T], in_=acc[:, 0:T], func=AF.Copy, scale=gs),
                         reads=[ab], writes=[sbf])
                else:
                    P.op("dve", lambda e, st=st, acc=acc: e.tensor_scalar(out=st[:, 0:T], in0=acc[:, 0:T], scalar1=gs, scalar2=None, op0=ALU.mult),
                         reads=[ab], writes=[sbf])
                r.flip += 1
            else:
                P.op("dve", lambda e, st=st, acc=acc, cc=cc: e.tensor_scalar(out=st[:, 0:T], in0=acc[:, 0:T], scalar1=gs[:, cc:cc + 1], scalar2=None, op0=ALU.mult),
                     reads=[ab, k.cb], writes=[sbf])
            out_cb(h, cc, st, sbf)
            continue
        P.op("act", lambda e, acc=acc, cc=cc: e.activation(out=r.raw[:, cc, 0:T], in_=acc[:, 0:T], func=AF.Copy), reads=[ab], writes=[r.rawb])
        sq, sqb = r.sq.next()
        P.op("dve", lambda e, sq=sq, cc=cc: e.tensor_tensor(out=sq[:, 0:T], in0=r.raw[:, cc, 0:T], in1=r.raw[:, cc, 0:T], op=ALU.mult), reads=[r.rawb], writes=[sqb])
        CUT = int(os.environ.get("CUT", "9"))
        if CUT <= 1:
            continue
        if cc == 0:
            ssq, ssb = r.pssq.next()
            r.cur_ssq = (ssq, ssb)
        ssq, ssb = r.cur_ssq
        ones = k.ones if norm == "full" else k.ones_half
        P.op("pe", lambda e, ssq=ssq, sq=sq, cc=cc, ones=ones: e.matmul(ssq[:, 0:T], ones[:, :], sq[:, 0:T], start=(cc == 0), stop=(cc == cph - 1)),
             reads=[sqb, k.cb], writes=[ssb])
        if CUT <= 2:
            continue
        if cc == cph - 1:
            rstd_op(P, r.rstd[:, 0:T], ssq[:, 0:T], dim, [ssb], r.rstdb)
            if CUT <= 3:
                continue
            for c2 in range(cph):
                st, sbf = r.stg.next()
                P.op("dve", lambda e, st=st, c2=c2: e.scalar_tensor_tensor(out=st[:, 0:T], in0=r.raw[:, c2, 0:T], scalar=gs[:, c2:c2 + 1], in1=r.rstd[:, 0:T],
                                                                           op0=ALU.mult, op1=ALU.mult),
                     reads=[r.rawb, r.rstdb, k.cb], writes=[sbf])
                out_cb(h, c2, st, sbf)


def tm_cols(k, r, hT, hb, KCin, T, W, wbuf, col0, ncols, post, dst_cb, gain_bc=None, scale=1.0, dim=None):
    P = k.P
    if post == "norm":
        assert ncols <= 512 and ncols <= 2 * r.PW
        panels = []
        for c0 in range(0, ncols, r.PW):
            n = min(r.PW, ncols - c0)
            panels.append((c0, n) + load_panel(k, r, W, wbuf, KCin, col0 + c0, n))
        for tb in range(T // 128):
            acc, ab = r.pacc.next()
            for (c0, n, wp, wb) in panels:
                for kc in range(KCin):
                    P.op("pe", lambda e, acc=acc, wp=wp, kc=kc, tb=tb, n=n, c0=c0: e.matmul(acc[:, c0:c0 + n], hT[:, kc, tb * 128:(tb + 1) * 128], wp[:, kc, 0:n],
                                                                                          start=(kc == 0), stop=(kc == KCin - 1)),
                         reads=[wb, hb], writes=[ab])
            n = ncols
            cp_, cpb = r.stgf.next()
            sq, sqb = r.stgf.next()
            sm, smb = r.small.next()
            P.op("act", lambda e, acc=acc, cp_=cp_, n=n: e.activation(out=cp_[:, 0:n], in_=acc[:, 0:n], func=AF.Copy), reads=[ab], writes=[cpb])
            P.op("act", lambda e, cp_=cp_, sq=sq, n=n: e.activation(out=sq[:, 0:n], in_=cp_[:, 0:n], func=AF.Square), reads=[cpb], writes=[sqb])
            P.op("dve", lambda e, sq=sq, sm=sm, n=n: e.reduce_sum(out=sm[:, 0:1], in_=sq[:, 0:n], axis=AX.X), reads=[sqb], writes=[smb])
            rstd_op(P, sm[:, 1:2], sm[:, 0:1], dim, [smb], smb)
            st, sbf = r.stg.next()
            P.op("dve", lambda e, cp_=cp_, st=st, sm=sm, n=n: e.scalar_tensor_tensor(out=st[:, 0:n], in0=cp_[:, 0:n], scalar=sm[:, 1:2], in1=gain_bc[:, 0:n],
                                                                                  op0=ALU.mult, op1=ALU.mult),
                 reads=[cpb, smb, k.cb], writes=[sbf])
            dst_cb(tb, 0, n, st, sbf)
        return
    for c0 in range(0, ncols, r.PW):
        n = min(r.PW, ncols - c0)
        wp, wb = load_panel(k, r, W, wbuf, KCin, col0 + c0, n)
        for tb in range(T // 128):
            acc, ab = r.pacc.next()
            for kc in range(KCin):
                P.op("pe", lambda e, acc=acc, wp=wp, kc=kc, tb=tb, n=n: e.matmul(acc[:, 0:n], hT[:, kc, tb * 128:(tb + 1) * 128], wp[:, kc, 0:n],
                                                                              start=(kc == 0), stop=(kc == KCin - 1)),
                     reads=[wb, hb], writes=[ab])
            if post == "resid":
                dst_cb(tb, c0, n, acc, ab)
            elif post == "f32scale":
                st, sbf = r.stgf.next()
                P.op("act", lambda e, acc=acc, st=st, n=n: e.activation(out=st[:, 0:n], in_=acc[:, 0:n], func=AF.Copy, scale=scale), reads=[ab], writes=[sbf])
                dst_cb(tb, c0, n, st, sbf)
            else:
                st, sbf = r.stg.next()
                if r.flip % 2 == 0:
                    P.op("act", lambda e, acc=acc, st=st, n=n: e.activation(out=st[:, 0:n], in_=acc[:, 0:n], func=AF.Copy, scale=scale), reads=[ab], writes=[sbf])
                else:
                    P.op("dve", lambda e, acc=acc, st=st, n=n: e.tensor_scalar(out=st[:, 0:n], in0=acc[:, 0:n], scalar1=scale, scalar2=None, op0=ALU.mult), reads=[ab], writes=[sbf])
                r.flip += 1
                dst_cb(tb, c0, n, st, sbf)


def alloc_normT(k, ctx, tag, D):
    r = NormCtx()
    r.xin = Ring([k.sb(ctx, f"{tag}_xin{i}", [128, D], F32) for i in range(2)])
    r.hb16 = Ring([k.sb(ctx, f"{tag}_hb{i}", [128, D], BF16) for i in range(2)])
    r.small = Ring([k.sb(ctx, f"{tag}_nsm{i}", [128, 4], F32) for i in range(4)])
    r.ptr = Ring([k.ps(ctx, f"{tag}_ptr{i}", [128, 8, 128], BF16) for i in range(2)])
    r.flip = 0
    return r


def norm_transpose(k, r, xin, xb, gain_bc, D, hT, hb, t0, rows=128):
    P = k.P
    KC = D // 128
    sm, smb = r.small.next()
    h16, h16b = r.hb16.next()
    P.op("act", lambda e: e.activation(out=h16[0:rows, :], in_=xin[0:rows, :], func=AF.Square), reads=[xb], writes=[h16b])
    P.op("dve", lambda e: e.reduce_sum(out=sm[0:rows, 0:1], in_=h16[0:rows, :], axis=AX.X), reads=[h16b], writes=[smb])
    rstd_op(P, sm[0:rows, 1:2], sm[0:rows, 0:1], D, [smb], smb)
    P.op("dve", lambda e: e.scalar_tensor_tensor(out=h16[0:rows, :], in0=xin[0:rows, :], scalar=sm[0:rows, 1:2], in1=gain_bc[0:rows, :], op0=ALU.mult, op1=ALU.mult),
         reads=[xb, smb, k.cb], writes=[h16b])
    for k0 in range(0, KC, 8):
        nk = min(8, KC - k0)
        pt, ptb = r.ptr.next()
        for i in range(nk):
            P.op("pe", lambda e, pt=pt, i=i, k0=k0: e.transpose(pt[:, i, 0:rows], h16[0:rows, (k0 + i) * 128:(k0 + i + 1) * 128], k.ident[0:rows, 0:rows]),
                 reads=[h16b, k.cb], writes=[ptb])
        eng = "act" if r.flip % 2 == 0 else "dve"
        r.flip += 1
        if eng == "act":
            P.op("act", lambda e, pt=pt, k0=k0, nk=nk: e.activation(out=hT[:, k0:k0 + nk, t0:t0 + rows], in_=pt[:, 0:nk, 0:rows], func=AF.Copy), reads=[ptb], writes=[hb])
        else:
            P.op("dve", lambda e, pt=pt, k0=k0, nk=nk: e.tensor_copy(out=hT[:, k0:k0 + nk, t0:t0 + rows], in_=pt[:, 0:nk, 0:rows]), reads=[ptb], writes=[hb])


def pp_layout(c):
    lay = {}
    col = 0

    def add(name, n, mult):
        nonlocal col
        lay[name] = (col, n, mult)
        col += n
    add("ev_q_lat_g", c.A_QR // 128, 1.0)
    add("ev_q_g", c.A_KVR // 128, c.A_KVR ** -0.5)
    add("ev_kv_g", c.A_KVR // 128, 1.0)
    add("ev_kidx_g", 1, 1.0)
    add("od_c_q_g", 1, c.C_D ** -0.5)
    add("od_c_k_g", 1, 1.0)
    add("od_d_q_g", 1, c.D_QK ** -0.5)
    add("od_d_k_g", 1, 1.0)
    add("od_sub_g", c.D_DV // 128, 1.0)
    for l in range(2):
        add(f"x_q_g{l}", 1, c.X_D ** -0.5)
        add(f"x_k_g{l}", 1, 1.0)
    for l in range(2):
        for nm in ("f_conv_w0_", "f_conv_w1_", "f_conv_w2_", "f_conv_b_"):
            add(nm + str(l), cdiv(c.FF, 128), 1.0)
    add("od_lam", 4, 1.0)
    add("od_sinks_pp", c.C_H // 2, 1.0)
    add("bsel0", 1, 1.0)
    add("bsel1", 1, 1.0)
    add("mask_lo", 1, 1.0)
    add("mask_hi", 1, 1.0)
    col = cdiv(col, 32) * 32
    return lay, col


def gv_layout(c):
    names = ["mem_norm_g", "mix_norm_g0", "mix_norm_g1", "xattn_norm_g0", "xattn_norm_g1", "ffn_norm_g0", "ffn_norm_g1", "ev_kv_g",
             "od_sinks", "farbias_a", "farbias_d"]
    return {n: i for i, n in enumerate(names)}


def load_small(k, ctx):
    c, P = k.c, k.P
    lay, ncol = pp_layout(c)
    k.pp_lay = lay
    k.pp = k.sb(ctx, "pp_sb", [128, ncol], F32)
    src = k.ext_in("pp", [128, ncol])
    import os
    P.dma("sp", lambda e: e.dma_start(out=k.pp[:], in_=src[:, :]), writes=[Buf() if os.environ.get("NODEP") else k.cb])
    for name, (c0, n, mult) in lay.items():
        if mult != 1.0 and not os.environ.get("NOMULT"):
            P.op("dve", lambda e, c0=c0, n=n, mult=mult: e.tensor_scalar(out=k.pp[:, c0:c0 + n], in0=k.pp[:, c0:c0 + n], scalar1=mult, scalar2=None, op0=ALU.mult),
                 reads=[k.cb], writes=[k.cb])
    k.gv_lay = gv_layout(c)
    if not os.environ.get("NOGV"):
        k.gv = k.ext_in("gv", [len(k.gv_lay), c.D])


def ppc(k, name):
    c0, n, _ = k.pp_lay[name]
    return k.pp[:, c0:c0 + n]


def load_bc(k, tile, name, n):
    row = k.gv_lay[name]
    src = k.gv[row:row + 1, 0:n].partition_broadcast(128)
    k.P.dma("sp", lambda e: e.dma_start(out=tile[:, 0:n], in_=src), writes=[k.cb])


def store_fm(k, dst, row0, TL, g, T):
    def cb(h, cc, st, sbf):
        r0 = row0(h, cc)
        k.P.dma("pool", lambda e: e.dma_start(out=k.d(dst)[r0:r0 + 128, g * T:(g + 1) * T], in_=st[:, 0:T]), reads=[sbf], writes=[k.db(dst)])
    return cb


def store_tm(k, dst, g, T, colbase=0):
    def cb(tb, c0, n, st, sbf):
        t0 = g * T + tb * 128
        k.P.dma("pool", lambda e: e.dma_start(out=k.d(dst)[t0:t0 + 128, colbase + c0:colbase + c0 + n], in_=st[:, 0:n]), reads=[sbf], writes=[k.db(dst)])
    return cb


def phase_p1(k, layer, xname):
    c, P = k.c, k.P
    T, TL, KC = c.T, c.TL, c.KC
    L = str(layer)
    even = layer % 2 == 0
    wname = "w_ev_w_in" if even else "w_od_w_in"
    W, wbuf = k.d(wname), k.db(wname)
    RC = c.A_KVR // 128
    if even:
        k.scratch("QA", [c.A_H * RC * 128, TL]); k.scratch("QI", [c.I_H * 128, TL])
        k.scratch("CT", [RC * 128, TL]); k.scratch("CTM", [TL, c.A_KVR]); k.scratch("KIT", [128, TL])
        k.scratch("WI", [TL, c.I_H], F32)
        k.scratch("QB", [c.B_H * 128, TL]); k.scratch("KB", [c.B_H * 128, TL]); k.scratch("VB", [TL, c.B_H * c.B_D])
    else:
        k.scratch("QC", [c.C_H * c.C_D, TL]); k.scratch("KC", [c.C_KV * 128, TL]); k.scratch("VC", [TL, c.C_KV * c.C_D])
        k.scratch("QD", [c.D_H * 2 * 128, TL]); k.scratch("KD", [c.D_H * 2 * 128, TL]); k.scratch("VD", [TL, c.D_H * c.D_DV])
    with ExitStack() as ctx:
        nr = alloc_normT(k, ctx, "p1n" + L, c.D)
        pr = alloc_proj(k, ctx, T, max(KC, 8), "p1p" + L, PW=256)
        hT = k.sb(ctx, "p1hT" + L, [128, KC, T], BF16)
        hb = Buf()
        gbc = k.sb(ctx, "p1g" + L, [128, c.D], F32)
        load_bc(k, gbc, "mix_norm_g" + L, c.D)
        if even:
            kvg = k.sb(ctx, "p1kvg", [128, c.A_KVR], F32)
            load_bc(k, kvg, "ev_kv_g", c.A_KVR)
            QRC = c.A_QR // 128
            cqT = k.sb(ctx, "p1cqT", [128, QRC, T], BF16)
            cqb = Buf()
        xsrc = k.d(xname)
        for g in range(c.NG):
            for tb in range(c.TG):
                xin, xb = nr.xin.next()
                t0 = g * T + tb * 128
                P.dma("sp", lambda e, xin=xin, t0=t0: e.dma_start(out=xin[:], in_=xsrc[t0:t0 + 128, :]), reads=[k.db(xname)], writes=[xb])
                norm_transpose(k, nr, xin, xb, gbc, c.D, hT, hb, tb * 128)
            if even:
                P1CUT = int(os.environ.get("P1CUT", "99"))
                o = 0
                def cq_cb(h, cc, st, sbf):
                    P.op("act", lambda e: e.activation(out=cqT[:, cc, 0:T], in_=st[:, 0:T], func=AF.Copy), reads=[sbf], writes=[cqb])
                if P1CUT >= 1: fm_heads(k, pr, hT, hb, KC, T, W, wbuf, o, 1, QRC, "full", ppc(k, "ev_q_lat_g"), cq_cb, dim=c.A_QR)
                o += c.A_QR
                if P1CUT >= 2: fm_heads(k, pr, hT, hb, KC, T, W, wbuf, o, 1, RC, "full", ppc(k, "ev_kv_g"), store_fm(k, "CT", lambda h, cc: cc * 128, TL, g, T), dim=c.A_KVR)
                if P1CUT >= 3: tm_cols(k, pr, hT, hb, KC, T, W, wbuf, o, c.A_KVR, "norm", store_tm(k, "CTM", g, T), gain_bc=kvg, dim=c.A_KVR)
                o += c.A_KVR
                if P1CUT >= 4: fm_heads(k, pr, hT, hb, KC, T, W, wbuf, o, 1, 1, "full", ppc(k, "ev_kidx_g"), store_fm(k, "KIT", lambda h, cc: 0, TL, g, T), dim=c.I_D)
                o += c.I_D
                if P1CUT >= 5: tm_cols(k, pr, hT, hb, KC, T, W, wbuf, o, c.I_H, "f32scale", store_tm(k, "WI", g, T), scale=(c.I_H ** -0.5) * (c.I_D ** -0.5))
                o += c.I_H
                if P1CUT >= 6: fm_heads(k, pr, hT, hb, KC, T, W, wbuf, o, c.B_H, 1, None, float(c.B_D ** -0.5), store_fm(k, "QB", lambda h, cc: h * 128, TL, g, T))
                o += c.B_H * c.B_D
                if P1CUT >= 7: fm_heads(k, pr, hT, hb, KC, T, W, wbuf, o, c.B_H, 1, None, 1.0, store_fm(k, "KB", lambda h, cc: h * 128, TL, g, T))
                o += c.B_H * c.B_D
                if P1CUT >= 8: tm_cols(k, pr, hT, hb, KC, T, W, wbuf, o, c.B_H * c.B_D, "copy", store_tm(k, "VB", g, T))
                Wqb, wqbb = k.d("w_ev_w_qb"), k.db("w_ev_w_qb")
                if P1CUT >= 9: fm_heads(k, pr, cqT, cqb, QRC, T, Wqb, wqbb, 0, c.A_H, RC, "full", ppc(k, "ev_q_g"),
                         store_fm(k, "QA", lambda h, cc: (h * RC + cc) * 128, TL, g, T), dim=c.A_KVR)
                Wqi, wqib = k.d("w_ev_w_qi"), k.db("w_ev_w_qi")
                if P1CUT >= 10: fm_heads(k, pr, cqT, cqb, QRC, T, Wqi, wqib, 0, c.I_H, 1, None, 1.0, store_fm(k, "QI", lambda h, cc: h * 128, TL, g, T))
            else:
                o = 0
                fm_heads(k, pr, hT, hb, KC, T, W, wbuf, o, c.C_H // 2, 1, "half", ppc(k, "od_c_q_g"), store_fm(k, "QC", lambda h, cc: h * 128, TL, g, T), dim=c.C_D)
                o += c.C_H * c.C_D
                def ck_colmap(ci, n, o=o):
                    cm = []
                    for q in range(n // 128):
                        kv = ci + q
                        cm.append((q * 128, o + kv * c.C_D, c.C_D))
                        cm.append((q * 128 + 64, o + kv * c.C_D, c.C_D))
                    return cm
                fm_heads(k, pr, hT, hb, KC, T, W, wbuf, 0, c.C_KV, 1, "half", ppc(k, "od_c_k_g"), store_fm(k, "KC", lambda h, cc: h * 128, TL, g, T), dim=c.C_D,
                         colmap_fn=ck_colmap)
                o += c.C_KV * c.C_D
                tm_cols(k, pr, hT, hb, KC, T, W, wbuf, o, c.C_KV * c.C_D, "copy", store_tm(k, "VC", g, T))
                o += c.C_KV * c.C_D
                fm_heads(k, pr, hT, hb, KC, T, W, wbuf, o, c.D_H * 2, 1, "full", ppc(k, "od_d_q_g"), store_fm(k, "QD", lambda h, cc: h * 128, TL, g, T), dim=c.D_QK)
                o += c.D_H * 2 * c.D_QK
                fm_heads(k, pr, hT, hb, KC, T, W, wbuf, o, c.D_H * 2, 1, "full", ppc(k, "od_d_k_g"), store_fm(k, "KD", lambda h, cc: h * 128, TL, g, T), dim=c.D_QK)
                o += c.D_H * 2 * c.D_QK
                tm_cols(k, pr, hT, hb, KC, T, W, wbuf, o, c.D_H * c.D_DV, "copy", store_tm(k, "VD", g, T))
        P.end_phase()
    if even:
        for nm, shp in (("CT", [RC * 128, TL]), ("CTM", [TL, c.A_KVR]), ("KIT", [128, TL]), ("KB", [c.B_H * 128, TL]), ("VB", [TL, c.B_H * c.B_D])):
            gather4(k, nm, shp)
    else:
        for nm, shp in (("KC", [c.C_KV * 128, TL]), ("VC", [TL, c.C_KV * c.C_D]), ("KD", [c.D_H * 2 * 128, TL]), ("VD", [TL, c.D_H * c.D_DV])):
            gather4(k, nm, shp)


def t5_bucket_np(dist):
    n = np.maximum(dist, 0)
    nf = np.maximum(n, 1).astype(np.float32)
    large = 16 + (np.log(nf / 16) / math.log(128 / 16) * 16).astype(np.int32)
    return np.where(n < 16, n, np.minimum(large, 31))


def host_weights2d(c, inp):
    w = {
        "ev_w_in": inp["ev_w_in"][0], "ev_w_qb": inp["ev_w_qb"][0].reshape(c.A_QR, -1), "ev_w_qi": inp["ev_w_qi"][0].reshape(c.A_QR, -1),
        "ev_w_uv": inp["ev_w_uv"][0].reshape(c.A_H * c.A_KVR, c.A_DV), "ev_w_out": inp["ev_w_out"][0],
        "od_w_in": inp["od_w_in"][0], "od_w_out": inp["od_w_out"][0],
    }
    for l in range(2):
        w[f"x_wq{l}"] = inp["x_wq"][l].reshape(c.D, -1)
        w[f"x_wk{l}"] = inp["x_wk"][l].reshape(c.D, -1)
        w[f"x_wv{l}"] = inp["x_wv"][l].reshape(c.D, -1)
        w[f"x_wo{l}"] = inp["x_wo"][l]
        w[f"f_w_gate{l}"] = inp["f_w_gate"][l]
        w[f"f_w_up{l}"] = inp["f_w_up"][l]
        w[f"f_w_down{l}"] = inp["f_w_down"][l]
    return w


def host_pp(c, inp):
    lay, ncol = pp_layout(c)
    pp = np.zeros((128, ncol), np.float32)

    def put(name, v):
        c0, n, _ = lay[name]
        v = np.asarray(v, np.float32).reshape(-1)
        if v.size == 64:
            v = np.tile(v, 2)
        pp[:, c0:c0 + n] = v.reshape(n, 128).T
    put("ev_q_lat_g", inp["ev_q_lat_g"][0]); put("ev_q_g", inp["ev_q_g"][0]); put("ev_kv_g", inp["ev_kv_g"][0]); put("ev_kidx_g", inp["ev_kidx_g"][0])
    put("od_c_q_g", inp["od_c_q_g"][0]); put("od_c_k_g", inp["od_c_k_g"][0]); put("od_d_q_g", inp["od_d_q_g"][0]); put("od_d_k_g", inp["od_d_k_g"][0])
    put("od_sub_g", inp["od_sub_g"][0])
    for l in range(2):
        put(f"x_q_g{l}", inp["x_q_g"][l]); put(f"x_k_g{l}", inp["x_k_g"][l])
        for t in range(3):
            put(f"f_conv_w{t}_{l}", inp["f_conv_w"][l][t])
        put(f"f_conv_b_{l}", inp["f_conv_b"][l])
    put("od_lam", np.concatenate([inp["od_lam_q1"][0], inp["od_lam_k1"][0], inp["od_lam_q2"][0], inp["od_lam_k2"][0]]))
    c0_, n_, _ = lay["od_sinks_pp"]
    sk = np.asarray(inp["od_sinks"][0], np.float32).reshape(c.C_H // 2, 2)
    pp[:64, c0_:c0_ + n_] = sk[:, 0][None, :]
    pp[64:, c0_:c0_ + n_] = sk[:, 1][None, :]
    put("mask_lo", (np.arange(128) < 64).astype(np.float32)); put("mask_hi", (np.arange(128) >= 64).astype(np.float32))
    return pp


def host_gv(c, inp):
    lay = gv_layout(c)
    gv = np.zeros((len(lay), c.D), np.float32)

    def put(name, v):
        v = np.asarray(v, np.float32).reshape(-1)
        gv[lay[name], :v.size] = v
    put("mem_norm_g", inp["mem_norm_g"])
    for l in range(2):
        put(f"mix_norm_g{l}", inp["mix_norm_g"][l]); put(f"xattn_norm_g{l}", inp["xattn_norm_g"][l]); put(f"ffn_norm_g{l}", inp["ffn_norm_g"][l])
    put("ev_kv_g", inp["ev_kv_g"][0]); put("od_sinks", inp["od_sinks"][0])
    rb = inp["rel_bias"]
    put("farbias_a", rb[31, 0:c.A_H]); put("farbias_d", rb[31, c.A_H + c.C_H:c.A_H + c.C_H + c.D_H])
    return gv


def host_prep(c, inp):
    w2d = host_weights2d(c, inp)
    pp = host_pp(c, inp)
    gv = host_gv(c, inp)
    ident = np.eye(128, dtype=np.float32)
    maps = []
    for core in range(8):
        b, j = divmod(core, 4)
        lay_, _ = pp_layout(c)
        ppc_ = pp.copy()
        ppc_[:, lay_["bsel0"][0]] = 1.0 if b == 0 else 0.0
        ppc_[:, lay_["bsel1"][0]] = 1.0 if b == 1 else 0.0
        m = {"c_ident": ident, "pp": ppc_, "gv": gv}
        m.update(host_masks(c, inp, j))
        m["halo_sel"] = host_halo_sel(c, j)
        m["mem"] = np.ascontiguousarray(inp["mem"][b])
        m["x"] = np.ascontiguousarray(inp["x"][b].reshape(c.NBLK, 128, c.D)[j::4].reshape(c.TL, c.D))
        for name, (Kr, N) in big_weights(c).items():
            rs = Kr // 8
            m[name] = np.ascontiguousarray(w2d[name]) if NOCC else np.ascontiguousarray(w2d[name][core * rs:(core + 1) * rs])
        maps.append(m)
    return maps


BIG = 1.0e30


def load_near(k, ctx, name, H, far_name, tag, dup=1):
    P = k.P
    src = k.ext_in(name, [5 * H * 128, 128])
    nb = k.sb(ctx, tag + "_nb", [128, 5, H * dup, 128], BF16)
    if far_name is not None:
        far = k.sb(ctx, tag + "_far", [128, H], F32)
        load_bc(k, far, far_name, H)
    stg = Ring([k.sb(ctx, f"{tag}_nbs{i}", [128, 128], F32) for i in range(3)])
    for n in range(5):
        for h in range(H):
            st_, sbf = stg.next()
            s_ap = src[(n * H + h) * 128:(n * H + h + 1) * 128, :]
            P.dma("sp", lambda e, st_=st_, s_ap=s_ap: e.dma_start(out=st_[:], in_=s_ap), writes=[sbf])
            st = st_[:].rearrange("s (o t) -> s o t", o=1)

            class _V:
                pass
            for dd in range(dup):
                if far_name is not None:
                    P.op("dve", lambda e, st=st, n=n, h=h, dd=dd: e.tensor_scalar(out=nb[:, n, h * dup + dd, :], in0=st[:, 0, :], scalar1=far[:, h:h + 1], scalar2=None, op0=ALU.subtract),
                         reads=[sbf, k.cb], writes=[k.cb])
                else:
                    P.op("dve", lambda e, st=st, n=n, h=h, dd=dd: e.tensor_copy(out=nb[:, n, h * dup + dd, :], in_=st[:, 0, :]), reads=[sbf], writes=[k.cb])
    return nb


def load_mask4(k, ctx, name, tag):
    P = k.P
    src = k.ext_in(name, [5 * 128, 128])
    mk = k.sb(ctx, tag + "_mk", [128, 5, 4, 128], BF16)
    st = k.sb(ctx, tag + "_mks", [128, 5, 128], F32)
    sbf = Buf()
    P.dma("sp", lambda e: e.dma_start(out=st[:], in_=src[:, :].rearrange("(n s) t -> s n t", s=128)), writes=[sbf])
    for q in range(4):
        P.op("dve", lambda e, q=q: e.tensor_copy(out=mk[:, :, q, :], in_=st[:]), reads=[sbf], writes=[k.cb])
    return mk


def phase_p2a_even(k):
    c, P = k.c, k.P
    TL, MB = c.TL, c.MB
    RC = c.A_KVR // 128
    NR = c.TOPK // 8
    k.scratch("OT", [c.EVEN_OUT, TL])
    with ExitStack() as ctx:
        nb_a = load_near(k, ctx, "near_bias_a", c.A_H, "farbias_a", "p2a")
        mk_c = load_mask4(k, ctx, "mask_causal", "p2c")
        mk_s = load_mask4(k, ctx, "mask_strict", "p2s")
        uinc = k.sb(ctx, "uinc", [128, 128], BF16)
        uf = k.sb(ctx, "uincf", [128, 128], F32)
        usrc = k.ext_in("c_uinc", [128, 128])
        ub = Buf()
        P.dma("sp", lambda e: e.dma_start(out=uf[:], in_=usrc[:, :]), writes=[ub])
        P.op("dve", lambda e: e.tensor_copy(out=uinc[:], in_=uf[:]), reads=[ub], writes=[k.cb])
        onec = k.sb(ctx, "onec", [128, 1], F32)
        P.op("dve", lambda e: e.memset(onec[:], 1.0), writes=[k.cb])
        midx = k.sb(ctx, "midx", [128, 512], F32)
        msrc = k.ext_in("mask_idx", [128, 512])
        P.dma("sp", lambda e: e.dma_start(out=midx[:], in_=msrc[:, :]), writes=[k.cb])
        wuv = k.sb(ctx, "wuv", [128, c.A_H * RC, c.A_DV], BF16)
        dma3(P, "sp", wuv[:], k.d("w_ev_w_uv")[:, :].rearrange("(x p) d -> p x d", p=128), [k.db("w_ev_w_uv")], [k.cb])
        qi = k.sb(ctx, "qi", [128, c.I_H, 128], BF16); qib = Buf()
        wi = k.sb(ctx, "wi", [128, c.I_H], F32); wib = Buf()
        dg = k.sb(ctx, "dg", [128, c.I_H, 128], BF16); dgb = Buf()
        qa = k.sb(ctx, "qa", [128, c.A_H * RC, 128], BF16); qab = Buf()
        qb = k.sb(ctx, "qb", [128, c.B_H, 128], BF16); qbb = Buf()
        sc = k.sb(ctx, "sc", [128, MB * 512], F32); scb = Buf()
        sel = k.sb(ctx, "sel", [128, MB * 512], BF16); selb = Buf()
        m8 = k.sb(ctx, "m8", [128, 8], F32); m8b = Buf()
        kit = Ring([k.sb(ctx, f"kit{i}", [128, 4, 128], BF16) for i in range(2)])
        rel = Ring([k.sb(ctx, f"rel{i}", [128, 512], BF16) for i in range(3)])
        ctb = Ring([k.sb(ctx, f"ctb{i}", [128, RC, 128], BF16) for i in range(2)])
        ctm = Ring([k.sb(ctx, f"ctm{i}", [128, c.A_KVR], BF16) for i in range(2)])
        pt = Ring([k.sb(ctx, f"pt{i}", [128, 512], BF16) for i in range(2)])
        oln = k.sb(ctx, "oln", [128, RC, 512], BF16); olnb = Buf()
        rcp = k.sb(ctx, "rcp", [128, 512], F32); rcpb = Buf()
        ot = Ring([k.sb(ctx, f"ot{i}", [128, 512], BF16) for i in range(2)])
        kbt = Ring([k.sb(ctx, f"kbt{i}", [128, 4, 128], BF16) for i in range(2)])
        vbt = Ring([k.sb(ctx, f"vbt{i}", [128, 512], BF16) for i in range(2)])
        ee = Ring([k.sb(ctx, f"ee{i}", [128, 512], F32) for i in range(2)])
        zz = Ring([k.sb(ctx, f"zz{i}", [128, 512], F32) for i in range(2)])
        spb = Ring([k.sb(ctx, f"spb{i}", [128, 512], BF16) for i in range(2)])
        d2 = Ring([k.sb(ctx, f"d2{i}", [128, 512], F32) for i in range(2)])
        tt = Ring([k.sb(ctx, f"tt{i}", [128, 512], F32) for i in range(2)])
        aa = Ring([k.sb(ctx, f"aa{i}", [128, 512], BF16) for i in range(2)])
        carry = k.sb(ctx, "carry", [128, 512], F32); carb = Buf()
        pA = Ring([k.ps(ctx, f"pA{i}", [128, 512]) for i in range(2)])
        pB = Ring([k.ps(ctx, f"pB{i}", [128, 512]) for i in range(2)])
        pO = [k.ps(ctx, f"pO{i}", [128, 512]) for i in range(4)]
        pOb = [Buf() for _ in range(4)]
        assert RC <= 3 or True

        for m in range(MB):
            t0 = m * 128
            NKB = 4 * m + 4
            dma3(P, "sp", qi[:], k.d("QI")[:, t0:t0 + 128].rearrange("(h d) t -> d h t", d=128), [k.db("QI")], [qib])
            P.dma("sp", lambda e, t0=t0: e.dma_start(out=wi[:], in_=k.d("WI")[t0:t0 + 128, :]), reads=[k.db("WI")], writes=[wib])
            dma3(P, "sp", qa[:], k.d("QA")[:, t0:t0 + 128].rearrange("(x p) t -> p x t", p=128), [k.db("QA")], [qab])
            dma3(P, "sp", qb[:], k.d("QB")[:, t0:t0 + 128].rearrange("(h d) t -> d h t", d=128), [k.db("QB")], [qbb])
            for h in range(c.I_H):
                P.op("dve", lambda e, h=h: e.tensor_scalar(out=dg[:, h, :], in0=k.ident[:], scalar1=wi[:, h:h + 1], scalar2=None, op0=ALU.mult),
                     reads=[wib, k.cb], writes=[dgb])
            for sbi in range(m + 1):
                kt, ktb = kit.next()
                for n_ in range(4):
                    for (ap_, bf_, lo_, nn_) in gsegs(k, "KIT", n_, 0, 128):
                        P.dma("sp", lambda e, kt=kt, sbi=sbi, n_=n_, ap_=ap_: e.dma_start(out=kt[:, n_, :], in_=ap_[:, sbi * 128:(sbi + 1) * 128]), reads=[bf_], writes=[ktb])
                acc, accb = pB.next()
                for h in range(c.I_H):
                    ps, psb = pA.next()
                    P.op("pe", lambda e, ps=ps, kt=kt, h=h: e.matmul(ps[:], qi[:, h, :], kt[:].rearrange("d n s -> d (n s)"), start=True, stop=True),
                         reads=[qib, ktb], writes=[psb])
                    rl, rlb = rel.next()
                    if h % 2 == 0:
                        P.op("act", lambda e, ps=ps, rl=rl: e.activation(out=rl[:], in_=ps[:], func=AF.Relu), reads=[psb], writes=[rlb])
                    else:
                        P.op("dve", lambda e, ps=ps, rl=rl: e.tensor_scalar(out=rl[:], in0=ps[:], scalar1=0.0, scalar2=None, op0=ALU.max), reads=[psb], writes=[rlb])
                    P.op("pe", lambda e, acc=acc, rl=rl, h=h: e.matmul(acc[:], dg[:, h, :], rl[:], start=(h == 0), stop=(h == c.I_H - 1)),
                         reads=[dgb, rlb], writes=[accb])
                if sbi == m:
                    P.op("dve", lambda e, acc=acc, sbi=sbi: e.tensor_tensor(out=sc[:, sbi * 512:(sbi + 1) * 512], in0=acc[:], in1=midx[:], op=ALU.add),
                         reads=[accb, k.cb], writes=[scb])
                else:
                    P.op("act", lambda e, acc=acc, sbi=sbi: e.activation(out=sc[:, sbi * 512:(sbi + 1) * 512], in_=acc[:], func=AF.Copy), reads=[accb], writes=[scb])
            W = (m + 1) * 512
            for r in range(NR):
                P.op("dve", lambda e, W=W: e.max(out=m8[:], in_=sc[:, 0:W]), reads=[scb], writes=[m8b])
                P.op("dve", lambda e, W=W: e.match_replace(out=sc[:, 0:W], in_to_replace=m8[:], in_values=sc[:, 0:W], imm_value=-BIG), reads=[m8b, scb], writes=[scb])
            P.op("dve", lambda e, W=W: e.tensor_scalar(out=sel[:, 0:W], in0=sc[:, 0:W], scalar1=-1.0e29, scalar2=NEG, op0=ALU.is_gt, op1=ALU.mult), reads=[scb], writes=[selb])
            for hg in range(c.A_H // 4):
                first = True
                for kb in range(NKB):
                    sbi, n = divmod(kb, 4)
                    near = kb - (4 * m - 1)
                    cb_, cbb = ctb.next()
                    for (ap_, bf_, lo_, nn_) in gsegs(k, "CT", n, 0, RC * 128):
                        P.dma("sp", lambda e, cb_=cb_, sbi=sbi, ap_=ap_, lo_=lo_, nn_=nn_: e.dma_start(out=cb_[:, lo_ // 128:(lo_ + nn_) // 128, :],
                                                                                                 in_=ap_[:, sbi * 128:(sbi + 1) * 128].rearrange("(rc r) s -> r rc s", r=128)),
                              reads=[bf_], writes=[cbb])
                    cm_, cmb = ctm.next()
                    for (ap_, bf_, lo_, nn_) in gsegs(k, "CTM", n, sbi * 128, 128):
                        P.dma("sp", lambda e, cm_=cm_, ap_=ap_: e.dma_start(out=cm_[:], in_=ap_), reads=[bf_], writes=[cmb])
                    ps, psb = pA.next()
                    for rc in range(RC):
                        P.op("pe", lambda e, ps=ps, cb_=cb_, rc=rc, hg=hg: e.matmul(ps[:].rearrange("s (h t) -> s h t", h=4), cb_[:, rc, :],
                                                                                   qa[:].rearrange("p (h rc) t -> p h rc t", rc=RC)[:, hg * 4:(hg + 1) * 4, rc, :],
                                                                                   start=(rc == 0), stop=False),
                             reads=[cbb, qab], writes=[psb])
                    lastmm = near < 0
                    P.op("pe", lambda e, ps=ps, kb=kb, lastmm=lastmm: e.matmul(ps[:], sel[:, kb * 128:(kb + 1) * 128], k.ident4[:].rearrange("p q t -> p (q t)"), start=False, stop=lastmm),
                         reads=[selb, k.cb], writes=[psb])
                    if near >= 0:
                        P.op("pe", lambda e, ps=ps, near=near: e.matmul(ps[:], k.ident[:], mk_c[:, near, :, :].rearrange("s q t -> s (q t)"), start=False, stop=False),
                             reads=[k.cb], writes=[psb])
                        P.op("pe", lambda e, ps=ps, near=near, hg=hg: e.matmul(ps[:].rearrange("s (h t) -> s h t", h=4), k.ident[:], nb_a[:, near, hg * 4:(hg + 1) * 4, :], start=False, stop=True),
                             reads=[k.cb], writes=[psb])
                    p_, pb_ = pt.next()
                    P.op("act", lambda e, ps=ps, p_=p_: e.activation(out=p_[:], in_=ps[:], func=AF.Exp), reads=[psb], writes=[pb_])
                    last = kb == NKB - 1
                    for rc in range(RC):
                        P.op("pe", lambda e, rc=rc, cm_=cm_, p_=p_, first=first, last=last: e.matmul(pO[rc][:], cm_[:, rc * 128:(rc + 1) * 128], p_[:], start=first, stop=last),
                             reads=[cmb, pb_], writes=[pOb[rc]])
                    P.op("pe", lambda e, p_=p_, first=first, last=last: e.matmul(pB.tiles[0][:], k.ones[:], p_[:], start=first, stop=last), reads=[pb_, k.cb], writes=[pB.bufs[0]])
                    first = False
                P.op("dve", lambda e: e.reciprocal(out=rcp[:], in_=pB.tiles[0][:]), reads=[pB.bufs[0]], writes=[rcpb])
                for rc in range(RC):
                    P.op("dve", lambda e, rc=rc: e.tensor_tensor(out=oln[:, rc, :], in0=pO[rc][:], in1=rcp[:], op=ALU.mult), reads=[pOb[rc], rcpb], writes=[olnb])
                ps, psb = pA.next()
                for hh in range(4):
                    h = hg * 4 + hh
                    for rc in range(RC):
                        P.op("pe", lambda e, ps=ps, hh=hh, h=h, rc=rc: e.matmul(ps[:, hh * 128:(hh + 1) * 128], wuv[:, h * RC + rc, :], oln[:, rc, hh * 128:(hh + 1) * 128],
                                                                             start=(rc == 0), stop=(rc == RC - 1)),
                             reads=[olnb, k.cb], writes=[psb])
                o_, ob_ = ot.next()
                P.op("act", lambda e, ps=ps, o_=o_: e.activation(out=o_[:], in_=ps[:], func=AF.Copy), reads=[psb], writes=[ob_])
                P.dma("pool", lambda e, o_=o_, hg=hg, t0=t0: e.dma_start(out=k.d("OT")[hg * 512:(hg + 1) * 512, t0:t0 + 128].rearrange("(h d) t -> d h t", d=128),
                                                                          in_=o_[:].rearrange("d (h t) -> d h t", h=4)),
                      reads=[ob_], writes=[k.db("OT")])
            for hg in range(c.B_H // 4):
                P.op("dve", lambda e: e.memset(carry[:], 0.0), writes=[carb])
                P.op("pe", lambda e: e.matmul(pO[0][:], k.zeros[:, 0:128], k.zeros[:], start=True, stop=False), reads=[k.cb], writes=[pOb[0]])
                for kb in range(NKB - 1, -1, -1):
                    sbi, n = divmod(kb, 4)
                    near = kb - (4 * m - 1)
                    kt, ktb = kbt.next()
                    for (ap_, bf_, lo_, nn_) in gsegs(k, "KB", n, hg * 512, 512):
                        P.dma("sp", lambda e, kt=kt, sbi=sbi, ap_=ap_, lo_=lo_, nn_=nn_: e.dma_start(
                            out=kt[:, lo_ // 128:(lo_ + nn_) // 128, :], in_=ap_[:, sbi * 128:(sbi + 1) * 128].rearrange("(h d) s -> d h s", d=128)),
                            reads=[bf_], writes=[ktb])
                    vt, vtb = vbt.next()
                    for (ap_, bf_, lo_, nn_) in gsegs(k, "VB", n, sbi * 128, 128):
                        P.dma("sp", lambda e, vt=vt, ap_=ap_, hg=hg: e.dma_start(out=vt[:], in_=ap_[:, hg * 512:(hg + 1) * 512]), reads=[bf_], writes=[vtb])
                    ps, psb = pA.next()
                    if near >= 1:
                        P.op("pe", lambda e, ps=ps, near=near: e.matmul(ps[:], k.ident[:], mk_s[:, near, :, :].rearrange("s q t -> s (q t)"), start=True, stop=False, skip_group_check=True),
                             reads=[k.cb], writes=[psb])
                    for hh in range(4):
                        P.op("pe", lambda e, ps=ps, kt=kt, hh=hh, hg=hg, near=near: e.matmul(ps[:, hh * 128:(hh + 1) * 128], kt[:, hh, :], qb[:, hg * 4 + hh, :],
                                                                                         start=(hh == 0 and near < 1), stop=(hh == 3), skip_group_check=True),
                             reads=[ktb, qbb], writes=[psb])
                    z_, zb_ = zz.next()
                    P.op("act", lambda e, ps=ps, z_=z_: e.activation(out=z_[:], in_=ps[:], func=AF.Copy), reads=[psb], writes=[zb_])
                    e_, eb_ = ee.next()
                    P.op("act", lambda e, z_=z_, e_=e_: e.activation(out=e_[:], in_=z_[:], func=AF.Exp), reads=[zb_], writes=[eb_])
                    s_, sb_ = spb.next()
                    P.op("act", lambda e, e_=e_, s_=s_: e.activation(out=s_[:], in_=e_[:], func=AF.Ln, bias=onec[:, 0:1]), reads=[eb_, k.cb], writes=[sb_])
                    cu, cub = pB.next()
                    P.op("pe", lambda e, cu=cu, s_=s_: e.matmul(cu[:], uinc[:], s_[:], start=True, stop=True), reads=[sb_, k.cb], writes=[cub])
                    cbp, cbpb = pB.next()
                    P.op("pe", lambda e, cbp=cbp, s_=s_: e.matmul(cbp[:], k.ones[:], s_[:], start=True, stop=True), reads=[sb_, k.cb], writes=[cbpb])
                    d_, db_ = d2.next()
                    P.op("dve", lambda e, d_=d_, cu=cu: e.tensor_tensor(out=d_[:], in0=cu[:], in1=carry[:], op=ALU.add), reads=[cub, carb], writes=[db_])
                    P.op("dve", lambda e, cbp=cbp: e.tensor_tensor(out=carry[:], in0=cbp[:], in1=carry[:], op=ALU.add), reads=[cbpb, carb], writes=[carb])
                    t_, tb_ = tt.next()
                    P.op("dve", lambda e, t_=t_, z_=z_, d_=d_: e.tensor_tensor(out=t_[:], in0=z_[:], in1=d_[:], op=ALU.subtract), reads=[zb_, db_], writes=[tb_])
                    a_, ab_ = aa.next()
                    P.op("act", lambda e, a_=a_, t_=t_: e.activation(out=a_[:], in_=t_[:], func=AF.Exp), reads=[tb_], writes=[ab_])
                    for hh in range(4):
                        P.op("pe", lambda e, vt=vt, a_=a_, hh=hh, kb=kb: e.matmul(pO[0][:, hh * 128:(hh + 1) * 128], vt[:, hh * 128:(hh + 1) * 128], a_[:, hh * 128:(hh + 1) * 128],
                                                                               start=False, stop=False, skip_group_check=True),
                             reads=[vtb, ab_], writes=[pOb[0]])
                P.op("pe", lambda e: e.matmul(pO[0][:], k.zeros[:, 0:128], k.zeros[:], start=False, stop=True), reads=[k.cb], writes=[pOb[0]])
                o_, ob_ = ot.next()
                P.op("act", lambda e, o_=o_: e.activation(out=o_[:], in_=pO[0][:], func=AF.Copy), reads=[pOb[0]], writes=[ob_])
                r0 = c.A_H * c.A_DV + hg * 512
                P.dma("pool", lambda e, o_=o_, r0=r0, t0=t0: e.dma_start(out=k.d("OT")[r0:r0 + 512, t0:t0 + 128].rearrange("(h d) t -> d h t", d=128),
                                                                          in_=o_[:].rearrange("d (h t) -> d h t", h=4)),
                      reads=[ob_], writes=[k.db("OT")])
        P.end_phase()


def host_masks(c, inp, j):
    rb = np.asarray(inp["rel_bias"], np.float32)
    s = np.arange(128)[:, None]
    t = np.arange(128)[None, :]
    out = {}
    dist = [((j + 1 - n) * 128 + t - s) for n in range(5)]
    bk = [t5_bucket_np(d) for d in dist]
    def bias_tiles(col0, H):
        return np.ascontiguousarray(np.stack([np.stack([rb[bk[n], col0 + h] for h in range(H)]) for n in range(5)]).reshape(5 * H * 128, 128))
    out["near_bias_a"] = bias_tiles(0, c.A_H)
    out["near_bias_c"] = bias_tiles(c.A_H, c.C_H)
    out["near_bias_d"] = bias_tiles(c.A_H + c.C_H, c.D_H)
    out["mask_causal"] = np.concatenate([np.where(d >= 0, 0.0, NEG) for d in dist]).astype(np.float32)
    out["mask_strict"] = np.concatenate([np.where(d > 0, 0.0, NEG) for d in dist]).astype(np.float32)
    out["mask_window"] = np.concatenate([np.where((d >= 0) & (d < 128), 0.0, NEG) for d in dist]).astype(np.float32)
    tt_ = np.arange(128)[:, None]
    ss_ = np.arange(128)[None, :]
    out["mask_idx"] = np.concatenate([np.where((j - n) * 128 + tt_ - ss_ >= 0, 0.0, -BIG) for n in range(4)], axis=1).astype(np.float32)
    out["c_uinc"] = (np.arange(128)[:, None] >= np.arange(128)[None, :]).astype(np.float32)
    return out


def phase_p2a_odd(k, layer):
    c, P = k.c, k.P
    TL, MB = c.TL, c.MB
    lambda_init = 0.8 - 0.6 * math.exp(-0.3 * layer)
    NCP = c.C_H // 2
    GRP = c.C_H // c.C_KV
    NPR = c.D_H * 2
    DVC = c.D_DV // 128
    k.scratch("OT1", [c.ODD_OUT, TL])
    with ExitStack() as ctx:
        nb_c = load_near(k, ctx, "near_bias_c", c.C_H, None, "o_c")
        nb_d = load_near(k, ctx, "near_bias_d", c.D_H, "farbias_d", "o_d", dup=2)
        mk_w = load_mask4(k, ctx, "mask_window", "o_w")
        mk_c = load_mask4(k, ctx, "mask_causal", "o_k")
        ones_lo = k.sb(ctx, "ones_lo", [128, 128], BF16)
        ones_hi = k.sb(ctx, "ones_hi", [128, 128], BF16)
        P.op("dve", lambda e: e.memset(ones_lo[:], 0.0), writes=[k.cb]); P.op("dve", lambda e: e.memset(ones_lo[:, 0:64], 1.0), writes=[k.cb])
        P.op("dve", lambda e: e.memset(ones_hi[:], 0.0), writes=[k.cb]); P.op("dve", lambda e: e.memset(ones_hi[:, 64:128], 1.0), writes=[k.cb])
        esink = k.sb(ctx, "esink", [128, NCP], F32)
        P.op("act", lambda e: e.activation(out=esink[:], in_=ppc(k, "od_sinks_pp"), func=AF.Exp), reads=[k.cb], writes=[k.cb])
        onesf = k.sb(ctx, "onesf", [128, 128], F32)
        P.op("dve", lambda e: e.memset(onesf[:], 1.0), writes=[k.cb])
        lp = k.sb(ctx, "lamp", [128, 4], F32); lpb = Buf()
        lam = ppc(k, "od_lam")
        P.op("dve", lambda e: e.tensor_tensor(out=lp[:, 0:1], in0=lam[:, 0:1], in1=lam[:, 1:2], op=ALU.mult), reads=[k.cb], writes=[lpb])
        P.op("dve", lambda e: e.tensor_tensor(out=lp[:, 1:2], in0=lam[:, 2:3], in1=lam[:, 3:4], op=ALU.mult), reads=[k.cb, lpb], writes=[lpb])
        pl = k.ps(ctx, "plam", [128, 2]); plb = Buf()
        P.op("pe", lambda e: e.matmul(pl[:], onesf[:], lp[:, 0:2], start=True, stop=True), reads=[lpb, k.cb], writes=[plb])
        P.op("act", lambda e: e.activation(out=lp[:, 2:4], in_=pl[:], func=AF.Exp), reads=[plb, lpb], writes=[lpb])
        nlam = k.sb(ctx, "nlam", [128, 1], F32)
        P.op("dve", lambda e: e.tensor_tensor(out=nlam[:], in0=lp[:, 3:4], in1=lp[:, 2:3], op=ALU.subtract), reads=[lpb], writes=[k.cb])
        P.op("dve", lambda e: e.tensor_scalar(out=nlam[:], in0=nlam[:], scalar1=-lambda_init, scalar2=None, op0=ALU.add), reads=[k.cb], writes=[k.cb])
        subg = k.sb(ctx, "subg", [128, DVC], F32)
        P.op("dve", lambda e: e.tensor_scalar(out=subg[:], in0=ppc(k, "od_sub_g"), scalar1=(1.0 - lambda_init), scalar2=None, op0=ALU.mult), reads=[k.cb], writes=[k.cb])
        qc = k.sb(ctx, "qc", [128, NCP, 128], BF16); qcb = Buf()
        qpad = k.sb(ctx, "qpad", [128, c.C_H, 128], BF16); qpb = Buf()
        qd = k.sb(ctx, "qd", [128, NPR, 128], BF16); qdb = Buf()
        kct = Ring([k.sb(ctx, f"kct{i}", [128, c.C_KV, 128], BF16) for i in range(2)])
        vlo = [k.sb(ctx, f"vlo{i}", [128, c.C_KV, 128], BF16) for i in range(2)]
        vhi = [k.sb(ctx, f"vhi{i}", [128, c.C_KV, 128], BF16) for i in range(2)]
        vcb = [Buf(), Buf()]
        for i in range(2):
            P.op("dve", lambda e, i=i: e.memset(vlo[i][:], 0.0), writes=[vcb[i]])
            P.op("dve", lambda e, i=i: e.memset(vhi[i][:], 0.0), writes=[vcb[i]])
        kdt = Ring([k.sb(ctx, f"kdt{i}", [128, 4, 128], BF16) for i in range(2)])
        vdt = Ring([k.sb(ctx, f"vdt{i}", [128, 2, c.D_DV], BF16) for i in range(2)])
        pt = Ring([k.sb(ctx, f"opt{i}", [128, 512], BF16) for i in range(2)])
        den = k.sb(ctx, "oden", [128, 512], F32); denb = Buf()
        osb = k.sb(ctx, "osb", [128, 2, 512], F32); osbb = Buf()
        dif = k.sb(ctx, "dif", [128, DVC, 256], F32); difb = Buf()
        sq = Ring([k.sb(ctx, f"osq{i}", [128, 256], BF16) for i in range(2)])
        rstd = k.sb(ctx, "orstd", [128, 256], F32); rstdb = Buf()
        ot = Ring([k.sb(ctx, f"oot{i}", [128, 512], BF16) for i in range(3)])
        pA = Ring([k.ps(ctx, f"opA{i}", [128, 512]) for i in range(2)])
        pO = [k.ps(ctx, f"opO{i}", [128, 512]) for i in range(3)]
        pOb = [Buf() for _ in range(3)]
        pS = k.ps(ctx, "opS", [128, 256]); pSb = Buf()

        for m in range(MB):
            t0 = m * 128
            NKB = 4 * m + 4
            dma3(P, "sp", qc[:], k.d("QC")[:, t0:t0 + 128].rearrange("(x p) t -> p x t", p=128), [k.db("QC")], [qcb])
            dma3(P, "sp", qd[:], k.d("QD")[:, t0:t0 + 128].rearrange("(x p) t -> p x t", p=128), [k.db("QD")], [qdb])
            for cp in range(NCP):
                P.op("dve", lambda e, cp=cp: e.tensor_scalar(out=qpad[:, 2 * cp, :], in0=qc[:, cp, :], scalar1=ppc(k, "mask_lo"), scalar2=None, op0=ALU.mult), reads=[qcb, k.cb], writes=[qpb])
                P.op("dve", lambda e, cp=cp: e.tensor_scalar(out=qpad[:, 2 * cp + 1, :], in0=qc[:, cp, :], scalar1=ppc(k, "mask_hi"), scalar2=None, op0=ALU.mult), reads=[qcb, k.cb], writes=[qpb])
            near_list = [n for n in range(5) if 4 * m - 1 + n >= 0]
            if m == 0:
                k._kcn = [k.sb(ctx, f"kcn{n}", [128, c.C_KV, 128], BF16) for n in range(5)]
                k._kcnb = [Buf() for _ in range(5)]
                k._vln = [k.sb(ctx, f"vln{n}", [128, c.C_KV, 128], BF16) for n in range(5)]
                k._vhn = [k.sb(ctx, f"vhn{n}", [128, c.C_KV, 128], BF16) for n in range(5)]
                k._vnb = [Buf() for _ in range(5)]
                for n in range(5):
                    P.op("dve", lambda e, n=n: e.memset(k._vln[n][:], 0.0), writes=[k._vnb[n]])
                    P.op("dve", lambda e, n=n: e.memset(k._vhn[n][:], 0.0), writes=[k._vnb[n]])
            for n in near_list:
                kb = 4 * m - 1 + n
                sbi, rk = divmod(kb, 4)
                for (ap_, bf_, lo_, nn_) in gsegs(k, "KC", rk, 0, c.C_KV * 128):
                    P.dma("sp", lambda e, n=n, sbi=sbi, ap_=ap_, lo_=lo_, nn_=nn_: e.dma_start(out=k._kcn[n][:, lo_ // 128:(lo_ + nn_) // 128, :],
                                                                                          in_=ap_[:, sbi * 128:(sbi + 1) * 128].rearrange("(v d) s -> d v s", d=128)),
                          reads=[bf_], writes=[k._kcnb[n]])
                for (ap_, bf_, lo_, nn_) in gsegs(k, "VC", rk, sbi * 128, 128):
                    vsrc = ap_.rearrange("s (v d) -> s v d", d=c.C_D)
                    P.dma("sp", lambda e, n=n, vsrc=vsrc: e.dma_start(out=k._vln[n][:, :, 0:64], in_=vsrc), reads=[bf_], writes=[k._vnb[n]])
                    P.dma("sp", lambda e, n=n, vsrc=vsrc: e.dma_start(out=k._vhn[n][:, :, 64:128], in_=vsrc), reads=[bf_], writes=[k._vnb[n]])
            for hg in range(c.C_H // 4):
                P.op("pe", lambda e: e.matmul(pO[0][:], k.zeros[:, 0:128], k.zeros[:], start=True, stop=False), reads=[k.cb], writes=[pOb[0]])
                for n in near_list:
                    ps, psb = pA.next()
                    P.op("pe", lambda e, ps=ps, n=n: e.matmul(ps[:], k.ident[:], mk_w[:, n, :, :].rearrange("s q t -> s (q t)"), start=True, stop=False), reads=[k.cb], writes=[psb])
                    P.op("pe", lambda e, ps=ps, n=n, hg=hg: e.matmul(ps[:].rearrange("s (h t) -> s h t", h=4), k.ident[:], nb_c[:, n, hg * 4:(hg + 1) * 4, :], start=False, stop=False),
                         reads=[k.cb], writes=[psb])
                    for hh in range(4):
                        h = hg * 4 + hh
                        kv = h // GRP
                        P.op("pe", lambda e, ps=ps, n=n, hh=hh, h=h, kv=kv: e.matmul(ps[:, hh * 128:(hh + 1) * 128], k._kcn[n][:, kv, :], qpad[:, h, :], start=False, stop=(hh == 3)),
                             reads=[k._kcnb[n], qpb], writes=[psb])
                    p_, pb_ = pt.next()
                    P.op("act", lambda e, ps=ps, p_=p_: e.activation(out=p_[:], in_=ps[:], func=AF.Exp), reads=[psb], writes=[pb_])
                    for hh in range(4):
                        h = hg * 4 + hh
                        kv = h // GRP
                        vt = k._vln[n] if hh % 2 == 0 else k._vhn[n]
                        on = ones_lo if hh % 2 == 0 else ones_hi
                        cpl = hh // 2
                        P.op("pe", lambda e, vt=vt, kv=kv, p_=p_, hh=hh, cpl=cpl: e.matmul(pO[0][:, cpl * 128:(cpl + 1) * 128], vt[:, kv, :], p_[:, hh * 128:(hh + 1) * 128],
                                                                                         start=False, stop=False, skip_group_check=True),
                             reads=[k._vnb[n], pb_], writes=[pOb[0]])
                        P.op("pe", lambda e, on=on, p_=p_, hh=hh, cpl=cpl: e.matmul(pO[0][:, 256 + cpl * 128:256 + (cpl + 1) * 128], on[:], p_[:, hh * 128:(hh + 1) * 128],
                                                                                  start=False, stop=False, skip_group_check=True),
                             reads=[k.cb, pb_], writes=[pOb[0]])
                P.op("pe", lambda e: e.matmul(pO[0][:], k.zeros[:, 0:128], k.zeros[:], start=False, stop=True), reads=[k.cb], writes=[pOb[0]])
                o_, ob_ = ot.next()
                for cpl in range(2):
                    cp = hg * 2 + cpl
                    P.op("dve", lambda e, cpl=cpl, cp=cp: e.tensor_scalar(out=den[:, cpl * 128:(cpl + 1) * 128], in0=pO[0][:, 256 + cpl * 128:256 + (cpl + 1) * 128],
                                                                        scalar1=esink[:, cp:cp + 1], scalar2=None, op0=ALU.add), reads=[pOb[0], k.cb], writes=[denb])
                P.op("dve", lambda e: e.reciprocal(out=den[:, 0:256], in_=den[:, 0:256]), reads=[denb], writes=[denb])
                P.op("dve", lambda e, o_=o_: e.tensor_tensor(out=o_[:, 0:256], in0=pO[0][:, 0:256], in1=den[:, 0:256], op=ALU.mult), reads=[pOb[0], denb], writes=[ob_])
                P.dma("pool", lambda e, o_=o_, hg=hg, t0=t0: e.dma_start(out=k.d("OT1")[hg * 256:(hg + 1) * 256, t0:t0 + 128].rearrange("(x d) t -> d x t", d=128),
                                                                          in_=o_[:, 0:256].rearrange("d (x t) -> d x t", x=2)),
                      reads=[ob_], writes=[k.db("OT1")])
            for pg in range(NPR // 4):
                for a in range(2):
                    P.op("pe", lambda e, a=a: e.matmul(pO[a][:], k.zeros[:, 0:128], k.zeros[:], start=True, stop=False), reads=[k.cb], writes=[pOb[a]])
                first = True
                for kb in range(NKB):
                    sbi, rk = divmod(kb, 4)
                    near = kb - (4 * m - 1)
                    kt, ktb = kdt.next()
                    for (ap_, bf_, lo_, nn_) in gsegs(k, "KD", rk, pg * 512, 512):
                        P.dma("sp", lambda e, kt=kt, sbi=sbi, ap_=ap_, lo_=lo_, nn_=nn_: e.dma_start(
                            out=kt[:, lo_ // 128:(lo_ + nn_) // 128, :], in_=ap_[:, sbi * 128:(sbi + 1) * 128].rearrange("(x d) s -> d x s", d=128)),
                            reads=[bf_], writes=[ktb])
                    vt, vtb = vdt.next()
                    for (ap_, bf_, lo_, nn_) in gsegs(k, "VD", rk, sbi * 128, 128):
                        P.dma("sp", lambda e, vt=vt, ap_=ap_, pg=pg: e.dma_start(
                            out=vt[:], in_=ap_[:, pg * 2 * c.D_DV:(pg + 1) * 2 * c.D_DV].rearrange("s (h d) -> s h d", h=2)),
                            reads=[bf_], writes=[vtb])
                    ps, psb = pA.next()
                    if near >= 0:
                        P.op("pe", lambda e, ps=ps, near=near: e.matmul(ps[:], k.ident[:], mk_c[:, near, :, :].rearrange("s q t -> s (q t)"), start=True, stop=False), reads=[k.cb], writes=[psb])
                        P.op("pe", lambda e, ps=ps, near=near, pg=pg: e.matmul(ps[:].rearrange("s (h t) -> s h t", h=4), k.ident[:], nb_d[:, near, pg * 4:(pg + 1) * 4, :], start=False, stop=False),
                             reads=[k.cb], writes=[psb])
                    for pp_ in range(4):
                        P.op("pe", lambda e, ps=ps, kt=kt, pp_=pp_, pg=pg, near=near: e.matmul(ps[:, pp_ * 128:(pp_ + 1) * 128], kt[:, pp_, :], qd[:, pg * 4 + pp_, :],
                                                                                           start=(pp_ == 0 and near < 0), stop=(pp_ == 3)),
                             reads=[ktb, qdb], writes=[psb])
                    p_, pb_ = pt.next()
                    P.op("act", lambda e, ps=ps, p_=p_: e.activation(out=p_[:], in_=ps[:], func=AF.Exp), reads=[psb], writes=[pb_])
                    for pp_ in range(4):
                        hl = pp_ // 2
                        for dv in range(DVC):
                            P.op("pe", lambda e, vt=vt, hl=hl, dv=dv, p_=p_, pp_=pp_: e.matmul(pO[dv][:, pp_ * 128:(pp_ + 1) * 128], vt[:, hl, dv * 128:(dv + 1) * 128], p_[:, pp_ * 128:(pp_ + 1) * 128],
                                                                                             start=False, stop=False, skip_group_check=True),
                                 reads=[vtb, pb_], writes=[pOb[dv]])
                    lastd = (kb == NKB - 1)
                    P.op("pe", lambda e, p_=p_, first=first, lastd=lastd: e.matmul(pO[2][:], k.ones[:], p_[:], start=first, stop=lastd), reads=[pb_, k.cb], writes=[pOb[2]])
                    first = False
                for a in range(2):
                    P.op("pe", lambda e, a=a: e.matmul(pO[a][:], k.zeros[:, 0:128], k.zeros[:], start=False, stop=True), reads=[k.cb], writes=[pOb[a]])
                P.op("dve", lambda e: e.reciprocal(out=den[:], in_=pO[2][:]), reads=[pOb[2]], writes=[denb])
                for dv in range(DVC):
                    P.op("dve", lambda e, dv=dv: e.tensor_tensor(out=osb[:, dv, :], in0=pO[dv][:], in1=den[:], op=ALU.mult), reads=[pOb[dv], denb], writes=[osbb])
                for hl in range(2):
                    for dv in range(DVC):
                        P.op("dve", lambda e, hl=hl, dv=dv: e.scalar_tensor_tensor(out=dif[:, dv, hl * 128:(hl + 1) * 128], in0=osb[:, dv, (2 * hl + 1) * 128:(2 * hl + 2) * 128], scalar=nlam[:, 0:1],
                                                                                   in1=osb[:, dv, (2 * hl) * 128:(2 * hl + 1) * 128], op0=ALU.mult, op1=ALU.add),
                             reads=[osbb, k.cb], writes=[difb])
                for dv in range(DVC):
                    q_, qb_ = sq.next()
                    P.op("act", lambda e, q_=q_, dv=dv: e.activation(out=q_[:], in_=dif[:, dv, :], func=AF.Square), reads=[difb], writes=[qb_])
                    P.op("pe", lambda e, q_=q_, dv=dv: e.matmul(pS[:], k.ones[:], q_[:], start=(dv == 0), stop=(dv == DVC - 1)), reads=[qb_, k.cb], writes=[pSb])
                rstd_op(P, rstd[:], pS[:], c.D_DV, [pSb], rstdb)
                o_, ob_ = ot.next()
                for dv in range(DVC):
                    P.op("dve", lambda e, o_=o_, dv=dv: e.scalar_tensor_tensor(out=o_[:, dv * 256:(dv + 1) * 256], in0=dif[:, dv, :], scalar=subg[:, dv:dv + 1], in1=rstd[:],
                                                                               op0=ALU.mult, op1=ALU.mult), reads=[difb, rstdb, k.cb], writes=[ob_])
                r0 = c.C_H * c.C_D + pg * 2 * c.D_DV
                for dv in range(DVC):
                    dst = k.d("OT1")[r0:r0 + 2 * c.D_DV, t0:t0 + 128].rearrange("(hl dv d) t -> d dv hl t", d=128, dv=DVC)[:, dv, :, :]
                    P.dma("pool", lambda e, o_=o_, dst=dst, dv=dv: e.dma_start(out=dst, in_=o_[:, dv * 256:(dv + 1) * 256].rearrange("d (hl t) -> d hl t", hl=2)),
                          reads=[ob_], writes=[k.db("OT1")])
        P.end_phase()


def resid_cb(k, pieces, src_name, dst_name, g, T):
    P = k.P

    def cb(tb, c0, n, acc, ab):
        t0 = g * T + tb * 128
        pin, pinb = pieces.next()
        P.dma("sp", lambda e: e.dma_start(out=pin[:, 0:n], in_=k.d(src_name)[t0:t0 + 128, c0:c0 + n]), reads=[k.db(src_name)], writes=[pinb])
        P.op("dve", lambda e: e.tensor_tensor(out=pin[:, 0:n], in0=acc[:, 0:n], in1=pin[:, 0:n], op=ALU.add), reads=[ab, pinb], writes=[pinb])
        P.dma("pool", lambda e: e.dma_start(out=k.d(dst_name)[t0:t0 + 128, c0:c0 + n], in_=pin[:, 0:n]), reads=[pinb], writes=[k.db(dst_name)])
    return cb


def phase_p2b(k, layer, xname, otname, nout, woutname):
    c, P = k.c, k.P
    T, TL, KC = c.T, c.TL, c.KC
    L = str(layer)
    NOC = nout // 128
    X1, X2 = "X1_" + L, "X2_" + L
    k.scratch(X1, [TL, c.D], F32); k.scratch(X2, [TL, c.D], F32)
    k.scratch("HFT" + L, [c.D, TL]); k.scratch("HALO" + L, [c.MB * 2, c.D])
    NM = c.MEM // 128
    with ExitStack() as ctx:
        nr = alloc_normT(k, ctx, "p2n" + L, c.D)
        pr = alloc_proj(k, ctx, max(T, c.MEM), max(KC, NOC), "p2p" + L, PW=256, RAWC=1)
        pieces = Ring([k.sb(ctx, f"p2pc{L}{i}", [128, 256], F32) for i in range(3)])
        hT = k.sb(ctx, "p2hT" + L, [128, max(KC, NOC), max(T, c.MEM)], BF16); hb = Buf()
        ott, ottb = hT, hb
        gx = k.sb(ctx, "p2gx" + L, [128, c.D], F32)
        gf = k.sb(ctx, "p2gf" + L, [128, c.D], F32)
        load_bc(k, gx, "xattn_norm_g" + L, c.D)
        qx = k.sb(ctx, "p2qx" + L, [128, c.X_H, T], BF16); qxb = Buf()
        kx = k.sb(ctx, "p2kx" + L, [128, c.X_H, c.MEM], BF16); kxb = Buf()
        vx = k.sb(ctx, "p2vx" + L, [128, NM, c.X_H * c.X_D], BF16); vxb = Buf()
        oxT = k.sb(ctx, "p2ox" + L, [128, c.X_H, T], BF16); oxb = Buf()
        ptx = Ring([k.sb(ctx, f"p2pt{L}{i}", [128, 512], BF16) for i in range(2)])
        rc_ = k.sb(ctx, "p2rc" + L, [128, 512], F32); rcb = Buf()
        pX = Ring([k.ps(ctx, f"p2pX{L}{i}", [128, 512]) for i in range(1)])
        pXo = k.ps(ctx, "p2pXo" + L, [128, 512]); pXob = Buf()
        pXd = k.ps(ctx, "p2pXd" + L, [128, 512]); pXdb = Buf()
        gm = gf
        load_bc(k, gm, "mem_norm_g", c.D)
        hmT, hmb = hT, hb
        for mt in range(NM):
            xin, xb = nr.xin.next()
            P.dma("sp", lambda e, xin=xin, mt=mt: e.dma_start(out=xin[:], in_=k.d("mem")[mt * 128:(mt + 1) * 128, :]), writes=[xb])
            norm_transpose(k, nr, xin, xb, gm, c.D, hmT, hmb, mt * 128)
        prm = pr

        def kx_cb(h, cc, st, sbf):
            P.op("act", lambda e: e.activation(out=kx[:, h, :], in_=st[:, 0:c.MEM], func=AF.Copy), reads=[sbf], writes=[kxb])
        fm_heads(k, prm, hmT, hmb, KC, c.MEM, k.d("w_x_wk" + L), k.db("w_x_wk" + L), 0, c.X_H, 1, "full", ppc(k, "x_k_g" + L), kx_cb, dim=c.X_D)

        def vx_cb(tb, c0, n, st, sbf):
            P.op("act", lambda e: e.activation(out=vx[:, tb, c0:c0 + n], in_=st[:, 0:n], func=AF.Copy), reads=[sbf], writes=[vxb])
        tm_cols(k, prm, hmT, hmb, KC, c.MEM, k.d("w_x_wv" + L), k.db("w_x_wv" + L), 0, c.X_H * c.X_D, "copy", vx_cb)
        load_bc(k, gf, "ffn_norm_g" + L, c.D)

        for g in range(c.NG):
            dma3(P, "sp", ott[:, 0:NOC, 0:T], k.d(otname)[:, g * T:(g + 1) * T].rearrange("(x p) t -> p x t", p=128), [k.db(otname)], [ottb], step=4)
            tm_cols(k, pr, ott, ottb, NOC, T, k.d(woutname), k.db(woutname), 0, c.D, "resid", resid_cb(k, pieces, xname, X1, g, T))
            for tb in range(c.TG):
                xin, xb = nr.xin.next()
                t0 = g * T + tb * 128
                P.dma("sp", lambda e, xin=xin, t0=t0: e.dma_start(out=xin[:], in_=k.d(X1)[t0:t0 + 128, :]), reads=[k.db(X1)], writes=[xb])
                norm_transpose(k, nr, xin, xb, gx, c.D, hT, hb, tb * 128)

            def qx_cb(h, cc, st, sbf):
                P.op("act", lambda e: e.activation(out=qx[:, h, 0:T], in_=st[:, 0:T], func=AF.Copy), reads=[sbf], writes=[qxb])
            fm_heads(k, pr, hT, hb, KC, T, k.d("w_x_wq" + L), k.db("w_x_wq" + L), 0, c.X_H, 1, "full", ppc(k, "x_q_g" + L), qx_cb, dim=c.X_D)
            for h in range(c.X_H):
                for mc in range(NM):
                    ps, psb = pX.next()
                    P.op("pe", lambda e, ps=ps, h=h, mc=mc: e.matmul(ps[:, 0:T], kx[:, h, mc * 128:(mc + 1) * 128], qx[:, h, 0:T], start=True, stop=True), reads=[kxb, qxb], writes=[psb])
                    p_, pb_ = ptx.next()
                    P.op("act", lambda e, ps=ps, p_=p_: e.activation(out=p_[:, 0:T], in_=ps[:, 0:T], func=AF.Exp), reads=[psb], writes=[pb_])
                    P.op("pe", lambda e, p_=p_, h=h, mc=mc: e.matmul(pXo[:, 0:T], vx[:, mc, h * c.X_D:(h + 1) * c.X_D], p_[:, 0:T], start=(mc == 0), stop=(mc == NM - 1)),
                         reads=[vxb, pb_], writes=[pXob])
                    P.op("pe", lambda e, p_=p_, mc=mc: e.matmul(pXd[:, 0:T], k.ones[:], p_[:, 0:T], start=(mc == 0), stop=(mc == NM - 1)), reads=[k.cb, pb_], writes=[pXdb])
                P.op("dve", lambda e: e.reciprocal(out=rc_[:, 0:T], in_=pXd[:, 0:T]), reads=[pXdb], writes=[rcb])
                P.op("dve", lambda e, h=h: e.tensor_tensor(out=oxT[:, h, 0:T], in0=pXo[:, 0:T], in1=rc_[:, 0:T], op=ALU.mult), reads=[pXob, rcb], writes=[oxb])
            tm_cols(k, pr, oxT, oxb, c.X_H, T, k.d("w_x_wo" + L), k.db("w_x_wo" + L), 0, c.D, "resid", resid_cb(k, pieces, X1, X2, g, T))
            for tb in range(c.TG):
                xin, xb = nr.xin.next()
                t0 = g * T + tb * 128
                P.dma("sp", lambda e, xin=xin, t0=t0: e.dma_start(out=xin[:], in_=k.d(X2)[t0:t0 + 128, :]), reads=[k.db(X2)], writes=[xb])
                norm_transpose(k, nr, xin, xb, gf, c.D, hT, hb, tb * 128)
                h16, h16b = nr.hb16.tiles[(nr.hb16.i - 1) % 2], nr.hb16.bufs[(nr.hb16.i - 1) % 2]
                mloc = g * c.TG + tb
                P.dma("pool", lambda e, h16=h16, mloc=mloc: e.dma_start(out=k.d("HALO" + L)[mloc * 2:mloc * 2 + 2, :], in_=h16[126:128, :]), reads=[h16b], writes=[k.db("HALO" + L)])
            dma3(P, "pool", k.d("HFT" + L)[:, g * T:(g + 1) * T].rearrange("(x p) t -> p x t", p=128), hT[:, 0:KC, 0:T], [hb], [k.db("HFT" + L)], step=4)
        P.end_phase()
    gather4(k, "HALO" + L, [c.MB * 2, c.D])


def phase_p3(k, layer, outname):
    c, P = k.c, k.P
    T, TL, KC, TG = c.T, c.TL, c.KC, c.TG
    L = str(layer)
    FC = c.FF // 128
    halves = [list(range(0, cdiv(FC, 2))), list(range(cdiv(FC, 2), FC))]
    FCH = len(halves[0])
    NCAND = 4 * (TG + 1) * 2
    X2 = "X2_" + L
    Wg, Wu, Wd = k.d("w_f_w_gate" + L), k.d("w_f_w_up" + L), k.d("w_f_w_down" + L)
    wgb, wub, wdb = k.db("w_f_w_gate" + L), k.db("w_f_w_up" + L), k.db("w_f_w_down" + L)
    with ExitStack() as ctx:
        hT = k.sb(ctx, "p3hT" + L, [128, KC, T], BF16); hb = Buf()
        hh = k.sb(ctx, "p3hh" + L, [128, KC, 2 * TG], BF16); hhb = Buf()
        cand = k.sb(ctx, "p3cand" + L, [128, c.D], BF16); candb = Buf()
        selh = k.sb(ctx, "p3selh" + L, [128, 2 * TG], BF16)
        self_ = k.sb(ctx, "p3self" + L, [128, 2 * TG], F32)
        if "halo_sel" not in k.dram:
            k.ext_in("halo_sel", [128, 2 * TG])
        sb_ = Buf()
        P.dma("sp", lambda e: e.dma_start(out=self_[:], in_=k.d("halo_sel")[:, :]), writes=[sb_])
        P.op("dve", lambda e: e.tensor_copy(out=selh[:], in_=self_[:]), reads=[sb_], writes=[k.cb])
        hid = k.sb(ctx, "p3hid" + L, [128, FCH, T], BF16); hidb = Buf()
        wgu = Ring([k.sb(ctx, f"p3wgu{L}{i}", [128, 2, KC, 128], BF16) for i in range(2)])
        wdn = Ring([k.sb(ctx, f"p3wd{L}{i}", [128, FCH, 128], BF16) for i in range(2)])
        gext = Ring([k.sb(ctx, f"p3ge{L}{i}", [128, TG, 130], F32) for i in range(2)])
        cacc = Ring([k.sb(ctx, f"p3ca{L}{i}", [128, TG, 128], F32) for i in range(2)])
        sil = Ring([k.sb(ctx, f"p3si{L}{i}", [128, T], F32) for i in range(2)])
        pieces = Ring([k.sb(ctx, f"p3pc{L}{i}", [128, 128], F32) for i in range(3)])
        pG = Ring([k.ps(ctx, f"p3pG{L}{i}", [128, 512]) for i in range(2)])
        pU = Ring([k.ps(ctx, f"p3pU{L}{i}", [128, 512]) for i in range(2)])
        pH = Ring([k.ps(ctx, f"p3pH{L}{i}", [128, 2 * TG]) for i in range(2)])
        pD = Ring([k.ps(ctx, f"p3pD{L}{i}", [128, 128]) for i in range(2)])
        w0, w1, w2 = (ppc(k, f"f_conv_w{t}_{L}") for t in range(3))
        bb = ppc(k, "f_conv_b_" + L)
        for g in range(c.NG):
            dma3(P, "sp", hT[:], k.d("HFT" + L)[:, g * T:(g + 1) * T].rearrange("(x p) t -> p x t", p=128), [k.db("HFT" + L)], [hb], step=4)
            lb0 = 1 if g == 0 else 0
            if g == 0:
                P.op("dve", lambda e: e.memset(cand[0:NCAND, :], 0.0), writes=[candb])
            for rk in range(4):
                r0 = rk * c.MB * 2 + (g * TG - 1 + lb0) * 2
                nrow = (TG + 1 - lb0) * 2
                d0 = rk * (TG + 1) * 2 + lb0 * 2
                for (ap_, bf_, lo_, nn_) in gsegs(k, "HALO" + L, rk, r0 - rk * c.MB * 2, nrow):
                    P.dma("sp", lambda e, ap_=ap_, d0=d0, lo_=lo_, nn_=nn_: e.dma_start(out=cand[d0 + lo_:d0 + lo_ + nn_, :], in_=ap_), reads=[bf_], writes=[candb])
            for kc in range(KC):
                ph, phb = pH.next()
                P.op("pe", lambda e, ph=ph, kc=kc: e.matmul(ph[:], cand[0:NCAND, kc * 128:(kc + 1) * 128], selh[0:NCAND, :], start=True, stop=True), reads=[candb, k.cb], writes=[phb])
                P.op("act", lambda e, ph=ph, kc=kc: e.activation(out=hh[:, kc, :], in_=ph[:], func=AF.Copy), reads=[phb], writes=[hhb])
            for hi, chunks in enumerate(halves):
                for p0 in range(0, len(chunks), 1):
                    pc = chunks[p0:p0 + 1]
                    n = len(pc) * 128
                    c0 = pc[0] * 128
                    wt, wtb = wgu.next()
                    dma3(P, "sp", wt[:, 0, :, 0:n], Wg[:, c0:c0 + n].rearrange("(kc p) n -> p kc n", p=128), [wgb], [wtb])
                    dma3(P, "sp", wt[:, 1, :, 0:n], Wu[:, c0:c0 + n].rearrange("(kc p) n -> p kc n", p=128), [wub], [wtb])
                    for q, fc in enumerate(pc):
                        lc = fc - chunks[0]
                        pg_, pgb = pG.next()
                        ph, phb = pH.next()
                        pu_, pub = pU.next()
                        for kc in range(KC):
                            P.op("pe", lambda e, pg_=pg_, wt=wt, q=q, kc=kc: e.matmul(pg_[:, 0:T], wt[:, 0, kc, q * 128:(q + 1) * 128], hT[:, kc, :], start=(kc == 0), stop=(kc == KC - 1)),
                                 reads=[wtb, hb], writes=[pgb])
                        for kc in range(KC):
                            P.op("pe", lambda e, ph=ph, wt=wt, q=q, kc=kc: e.matmul(ph[:], wt[:, 0, kc, q * 128:(q + 1) * 128], hh[:, kc, :], start=(kc == 0), stop=(kc == KC - 1)),
                                 reads=[wtb, hhb], writes=[phb])
                        for kc in range(KC):
                            P.op("pe", lambda e, pu_=pu_, wt=wt, q=q, kc=kc: e.matmul(pu_[:, 0:T], wt[:, 1, kc, q * 128:(q + 1) * 128], hT[:, kc, :], start=(kc == 0), stop=(kc == KC - 1)),
                                 reads=[wtb, hb], writes=[pub])
                        ge, geb = gext.next()
                        P.op("act", lambda e, ge=ge, pg_=pg_: e.activation(out=ge[:, :, 2:130], in_=pg_[:, 0:T].rearrange("p (b t) -> p b t", b=TG), func=AF.Copy), reads=[pgb], writes=[geb])
                        P.op("act", lambda e, ge=ge, ph=ph: e.activation(out=ge[:, :, 0:2], in_=ph[:].rearrange("p (b t) -> p b t", b=TG), func=AF.Copy), reads=[phb], writes=[geb])
                        ca, cab = cacc.next()
                        P.op("dve", lambda e, ca=ca, ge=ge, fc=fc: e.tensor_scalar(out=ca[:], in0=ge[:, :, 2:130], scalar1=w2[:, fc:fc + 1], scalar2=bb[:, fc:fc + 1], op0=ALU.mult, op1=ALU.add),
                             reads=[geb, k.cb], writes=[cab])
                        P.op("dve", lambda e, ca=ca, ge=ge, fc=fc: e.scalar_tensor_tensor(out=ca[:], in0=ge[:, :, 1:129], scalar=w1[:, fc:fc + 1], in1=ca[:], op0=ALU.mult, op1=ALU.add),
                             reads=[geb, k.cb, cab], writes=[cab])
                        P.op("dve", lambda e, ca=ca, ge=ge, fc=fc: e.scalar_tensor_tensor(out=ca[:], in0=ge[:, :, 0:128], scalar=w0[:, fc:fc + 1], in1=ca[:], op0=ALU.mult, op1=ALU.add),
                             reads=[geb, k.cb, cab], writes=[cab])
                        si, sib = sil.next()
                        P.op("act", lambda e, si=si, ca=ca: e.activation(out=si[:, 0:T], in_=ca[:].rearrange("p b t -> p (b t)"), func=AF.Silu), reads=[cab], writes=[sib])
                        P.op("dve", lambda e, si=si, pu_=pu_, lc=lc: e.tensor_tensor(out=hid[:, lc, :], in0=pu_[:, 0:T], in1=si[:, 0:T], op=ALU.mult), reads=[pub, sib], writes=[hidb])
                nch = len(chunks)
                r0 = chunks[0] * 128
                base = X2 if hi == 0 else outname
                for c0 in range(0, c.D, 128):
                    wt, wtb = wdn.next()
                    dma3(P, "sp", wt[:, 0:nch, :], Wd[r0:r0 + nch * 128, c0:c0 + 128].rearrange("(x p) n -> p x n", p=128), [wdb], [wtb])
                    for tb in range(TG):
                        pd, pdb = pD.next()
                        for lc in range(nch):
                            P.op("pe", lambda e, pd=pd, wt=wt, lc=lc, tb=tb, nch=nch: e.matmul(pd[:], hid[:, lc, tb * 128:(tb + 1) * 128], wt[:, lc, :], start=(lc == 0), stop=(lc == nch - 1)),
                                 reads=[hidb, wtb], writes=[pdb])
                        t0 = g * T + tb * 128
                        pin, pinb = pieces.next()
                        P.dma("sp", lambda e, pin=pin, t0=t0, c0=c0, base=base: e.dma_start(out=pin[:], in_=k.d(base)[t0:t0 + 128, c0:c0 + 128]), reads=[k.db(base)], writes=[pinb])
                        P.op("dve", lambda e, pin=pin, pd=pd: e.tensor_tensor(out=pin[:], in0=pd[:], in1=pin[:], op=ALU.add), reads=[pdb, pinb], writes=[pinb])
                        P.dma("pool", lambda e, pin=pin, t0=t0, c0=c0: e.dma_start(out=k.d(outname)[t0:t0 + 128, c0:c0 + 128], in_=pin[:]), reads=[pinb], writes=[k.db(outname)])
        P.end_phase()


def host_halo_sel(c, j):
    TG = c.TG
    sel = np.zeros((128, 2 * TG), np.float32)
    rk = (j - 1) % 4
    for tb in range(TG):
        lb = tb + 1 if j > 0 else tb
        for tok in range(2):
            sel[rk * (TG + 1) * 2 + lb * 2 + tok, tb * 2 + tok] = 1.0
    return sel


def build(c, debug=()):
    k = K(c, debug=debug)
    with ExitStack() as ctx:
        phase_consts(k, ctx)
        load_small(k, ctx)
        k.ext_in("x", [c.TL, c.D])
        k.ext_in("mem", [c.MEM, c.D])
        k.scratch("XL0", [c.TL, c.D], F32)
        k.ext_out("out", [c.TL, c.D])
        phase_weights(k)
        k.P.end_phase()
        STOP = int(os.environ.get("STOP", "99"))
        steps = [lambda: phase_p1(k, 0, "x"), lambda: phase_p2a_even(k), lambda: phase_p2b(k, 0, "x", "OT", c.EVEN_OUT, "w_ev_w_out"), lambda: phase_p3(k, 0, "XL0"),
                 lambda: phase_p1(k, 1, "XL0" if STOP > 4 else "x"), lambda: phase_p2a_odd(k, 1), lambda: phase_p2b(k, 1, "XL0", "OT1", c.ODD_OUT, "w_od_w_out"),
                 lambda: phase_p3(k, 1, "out")]
        SKIP = [int(v) for v in os.environ.get("SKIP", "").split(",") if v]
        for si, st_ in enumerate(steps):
            if si < STOP and si not in SKIP:
                st_()
        if STOP < 8 or SKIP:
            k.P.dma("pool", lambda e: e.dma_start(out=k.d("out")[0:128, :], in_=k.d("x")[0:128, :]), writes=[k.db("out")])
        if debug:
            k.dump_debug()
        k.P.final_wait("pool", [k.db("out")])
        k.P.end_phase()
    return k


def run(c, inp, debug=()):
    k = build(c, debug)
    maps = host_prep(c, inp)
    maps = [{n: m[n] for n in k.inputs} for m in maps]
    res = run_bass_kernel_spmd(k.nc, maps, core_ids=list(range(8)))
    return k, res


def assemble(c, results):
    out = np.zeros((c.B, c.S, c.D), np.float32)
    for core in range(8):
        b, j = divmod(core, 4)
        out[b].reshape(c.NBLK, 128, c.D)[j::4] = np.asarray(results[core]["out"], np.float32).reshape(c.MB, 128, c.D)
    return out


def kernel(**inputs):
    inp = {k_: np.asarray(v) for k_, v in inputs.items()}
    k, res = run(FULL, inp)
    return assemble(FULL, res.results)
```
